# Optimizing a Trainium2 kernel written in Bass

```python
import jax, jax.numpy as jnp
from jax import lax
import numpy as np

D_MODEL = 1024
BATCH = 2
SEQ = 16384
DEPTH = 2

HEAD_DIM = 64
FOX_HEADS = 8
NSA_HEADS = 8
NSA_KV_GROUPS = 2
NSA_GROUP_SIZE = NSA_HEADS // NSA_KV_GROUPS
CMP_BLOCK = 32
CMP_STRIDE = 16
SLC_BLOCK = 64
N_SELECT = 16
WINDOW = 512
Q_BLOCK = 128
POOL_WINDOWS = (2, 4, 8, 16)
POOL_GROUP = D_MODEL // len(POOL_WINDOWS)
FFN_HIDDEN = -(-8 * D_MODEL // (3 * 256)) * 256
RMS_EPS = 1e-6
NEG_INF = -1e30
BIG = 1e30
N_EVEN = (DEPTH + 1) // 2
N_ODD = DEPTH // 2
MIX_WIDTH = (FOX_HEADS + NSA_HEADS) * HEAD_DIM
KV_DIM = NSA_KV_GROUPS * HEAD_DIM
IN_SIZES = (FOX_HEADS * HEAD_DIM, FOX_HEADS * HEAD_DIM, FOX_HEADS * HEAD_DIM, FOX_HEADS,
            NSA_HEADS * HEAD_DIM, KV_DIM, KV_DIM, KV_DIM, KV_DIM, KV_DIM, KV_DIM, 3 * NSA_HEADS)
IN_COLS = sum(IN_SIZES)

kernel_name = "fox_nsa_pool_hybrid_trunk"


def rmsnorm(x, g):
    xf = x.astype(jnp.float32)
    y = xf * lax.rsqrt(jnp.mean(xf * xf, axis=-1, keepdims=True) + RMS_EPS)
    return (y * g.astype(jnp.float32)).astype(x.dtype)


def masked_softmax(s, mask):
    s = jnp.where(mask, s, NEG_INF)
    m = jnp.max(s, axis=-1, keepdims=True)
    e = jnp.where(mask, jnp.exp(s - m), 0.0)
    return e / jnp.maximum(jnp.sum(e, axis=-1, keepdims=True), 1e-30)


def alibi_slopes(n):
    return jnp.asarray(2.0 ** (-8.0 * (np.arange(n, dtype=np.float32) + 1.0) / n), dtype=jnp.float32)


def fox_attention(q, k, v, log_f):
    B, S, H, dh = q.shape
    c = jnp.transpose(jnp.cumsum(log_f, axis=1), (0, 2, 1))
    kpos = jnp.arange(S)
    scale = dh ** -0.5

    def block(qb):
        start = qb * Q_BLOCK
        t = start + jnp.arange(Q_BLOCK)
        qi = lax.dynamic_slice_in_dim(q, start, Q_BLOCK, axis=1)
        ci = lax.dynamic_slice_in_dim(c, start, Q_BLOCK, axis=2)
        s = jnp.einsum('bqhd,bkhd->bhqk', qi, k).astype(jnp.float32) * scale
        s = s + ci[..., None] - c[:, :, None, :]
        mask = kpos[None, :] <= t[:, None]
        p = masked_softmax(s, mask)
        return jnp.einsum('bhqk,bkhd->bqhd', p.astype(v.dtype), v)

    out = lax.map(block, jnp.arange(S // Q_BLOCK))
    return jnp.transpose(out, (1, 0, 2, 3, 4)).reshape(B, S, H * dh)


def nsa_compress(tok, pe, w1, w2):
    B, S, G, dh = tok.shape
    r = CMP_BLOCK // CMP_STRIDE
    n_chunks = S // CMP_STRIDE
    n_c = n_chunks - r + 1
    chunks = tok.reshape(B, n_chunks, CMP_STRIDE, G, dh)
    blocks = jnp.concatenate([chunks[:, j:j + n_c] for j in range(r)], axis=2)
    blocks = blocks + pe[None, None, :, None, :]
    flat = jnp.transpose(blocks, (0, 1, 3, 2, 4)).reshape(B, n_c, G, CMP_BLOCK * dh)
    return jax.nn.gelu(flat @ w1) @ w2


def nsa_attention(q, k_c, v_c, k_s, v_s, k_w, v_w, gates, pe, w1, w2):
    B, S, H, dh = q.shape
    G = k_c.shape[2]
    R = H // G
    scale = dh ** -0.5
    slopes = alibi_slopes(H).reshape(G, R)
    dt = q.dtype
    r = CMP_BLOCK // CMP_STRIDE
    ratio = SLC_BLOCK // CMP_STRIDE
    n_s = S // SLC_BLOCK
    n_sel = min(N_SELECT, n_s)

    kc = nsa_compress(k_c, pe[0], w1[0], w2[0])
    vc = nsa_compress(v_c, pe[1], w1[1], w2[1])
    n_c = kc.shape[1]
    cmp_end = jnp.arange(n_c) * CMP_STRIDE + CMP_BLOCK - 1

    q_g = q.reshape(B, S, G, R, dh)
    k_blocks = jnp.transpose(k_s.reshape(B, n_s, SLC_BLOCK, G, dh), (0, 3, 1, 2, 4)).reshape(B, G, n_s, SLC_BLOCK * dh)
    v_blocks = jnp.transpose(v_s.reshape(B, n_s, SLC_BLOCK, G, dh), (0, 3, 1, 2, 4)).reshape(B, G, n_s, SLC_BLOCK * dh)
    k_w_pad = jnp.pad(k_w, ((0, 0), (WINDOW, 0), (0, 0), (0, 0)))
    v_w_pad = jnp.pad(v_w, ((0, 0), (WINDOW, 0), (0, 0), (0, 0)))
    blk = jnp.arange(n_s)

    def block(qb):
        start = qb * Q_BLOCK
        t = start + jnp.arange(Q_BLOCK)
        tf = t.astype(jnp.float32)
        qi = lax.dynamic_slice_in_dim(q_g, start, Q_BLOCK, axis=1)
        gi = lax.dynamic_slice_in_dim(gates, start, Q_BLOCK, axis=1)

        d_c = tf[:, None] - cmp_end[None, :].astype(jnp.float32)
        s_c = jnp.einsum('bqgrd,bngd->bgrqn', qi, kc).astype(jnp.float32) * scale
        s_c = s_c - slopes[None, :, :, None, None] * d_c
        p_c = masked_softmax(s_c, d_c >= 0)
        o_c = jnp.einsum('bgrqn,bngd->bqgrd', p_c.astype(dt), vc)

        imp = jnp.sum(p_c, axis=2)
        imp = jnp.pad(imp, ((0, 0), (0, 0), (0, 0), (r - 1, ratio * n_s + r - 1 - n_c - (r - 1))))
        p_slc = imp[..., 0:ratio * n_s:ratio]
        for o in range(1, ratio + r - 1):
            p_slc = p_slc + imp[..., o:o + ratio * n_s:ratio]
        cur = t // SLC_BLOCK
        forced = (blk[None, :] == 0) | (blk[None, :] == cur[:, None]) | (blk[None, :] == cur[:, None] - 1)
        future = blk[None, :] > cur[:, None]
        p_slc = jnp.where(forced, BIG, jnp.where(future, NEG_INF, p_slc))
        _, idx = lax.top_k(p_slc, n_sel)

        flat_idx = idx.reshape(B, G, Q_BLOCK * n_sel)[..., None]
        kg = jnp.take_along_axis(k_blocks, flat_idx, axis=2).reshape(B, G, Q_BLOCK, n_sel * SLC_BLOCK, dh)
        vg = jnp.take_along_axis(v_blocks, flat_idx, axis=2).reshape(B, G, Q_BLOCK, n_sel * SLC_BLOCK, dh)
        kpos = (idx[..., None] * SLC_BLOCK + jnp.arange(SLC_BLOCK)).reshape(B, G, Q_BLOCK, n_sel * SLC_BLOCK)
        d_s = (t[None, None, :, None] - kpos).astype(jnp.float32)
        s_s = jnp.einsum('bqgrd,bgqkd->bgrqk', qi, kg).astype(jnp.float32) * scale
        s_s = s_s - slopes[None, :, :, None, None] * d_s[:, :, None]
        p_s = masked_softmax(s_s, (d_s >= 0)[:, :, None])
        o_s = jnp.einsum('bgrqk,bgqkd->bqgrd', p_s.astype(dt), vg)

        kw = lax.dynamic_slice_in_dim(k_w_pad, start, WINDOW + Q_BLOCK, axis=1)
        vw = lax.dynamic_slice_in_dim(v_w_pad, start, WINDOW + Q_BLOCK, axis=1)
        wpos = start - WINDOW + jnp.arange(WINDOW + Q_BLOCK)
        d_w = t[:, None] - wpos[None, :]
        m_w = (d_w >= 0) & (d_w < WINDOW) & (wpos[None, :] >= 0)
        s_w = jnp.einsum('bqgrd,bkgd->bgrqk', qi, kw).astype(jnp.float32) * scale
        s_w = s_w - slopes[None, :, :, None, None] * d_w.astype(jnp.float32)
        p_w = masked_softmax(s_w, m_w)
        o_w = jnp.einsum('bgrqk,bkgd->bqgrd', p_w.astype(dt), vw)

        return gi[..., 0:1] * o_c + gi[..., 1:2] * o_s + gi[..., 2:3] * o_w

    out = lax.map(block, jnp.arange(S // Q_BLOCK))
    return jnp.transpose(out, (1, 0, 2, 3, 4, 5)).reshape(B, S, H * dh)


def mixer_fox_nsa(h, w_in, b_f, cmp_pe, cmp_w1, cmp_w2, w_out):
    B, S, _ = h.shape
    proj = h @ w_in
    offs = [0]
    for n in IN_SIZES:
        offs.append(offs[-1] + n)
    fq, fk, fv, ff, nq, kc, vc, ks, vs, kw, vw, ng = [proj[..., offs[i]:offs[i + 1]] for i in range(len(IN_SIZES))]
    hd = (B, S, FOX_HEADS, HEAD_DIM)
    log_f = jax.nn.log_sigmoid((ff + b_f).astype(jnp.float32))
    o_fox = fox_attention(fq.reshape(hd), fk.reshape(hd), fv.reshape(hd), log_f)
    kvd = (B, S, NSA_KV_GROUPS, HEAD_DIM)
    gates = jax.nn.sigmoid(ng.astype(jnp.float32)).astype(h.dtype).reshape(B, S, NSA_KV_GROUPS, NSA_GROUP_SIZE, 3)
    o_nsa = nsa_attention(nq.reshape(B, S, NSA_HEADS, HEAD_DIM), kc.reshape(kvd), vc.reshape(kvd),
                          ks.reshape(kvd), vs.reshape(kvd), kw.reshape(kvd), vw.reshape(kvd),
                          gates, cmp_pe, cmp_w1, cmp_w2)
    return jnp.concatenate([o_fox, o_nsa], axis=-1) @ w_out


def mixer_pool(h, w_groups, scale):
    B, S, D = h.shape
    hf = h.astype(jnp.float32)
    c = jnp.cumsum(hf, axis=1)
    count = jnp.arange(1, S + 1, dtype=jnp.float32)
    outs = []
    for gi, w in enumerate(POOL_WINDOWS):
        sl = slice(gi * POOL_GROUP, (gi + 1) * POOL_GROUP)
        cg = c[..., sl]
        lag = jnp.pad(cg, ((0, 0), (w, 0), (0, 0)))[:, :S]
        mean = (cg - lag) / jnp.minimum(count, float(w))[None, :, None]
        outs.append(mean - hf[..., sl])
    pooled = jnp.stack(outs, axis=2).astype(h.dtype)
    y = jnp.einsum('bsgc,gcd->bsgd', pooled, w_groups).reshape(B, S, D)
    return y * scale


def swiglu(h, w_gate, w_up, w_down):
    return (jax.nn.silu(h @ w_gate) * (h @ w_up)) @ w_down


def setup_inputs(seed: int = 0) -> dict:
    key = jax.random.key(seed)
    ks = jax.random.split(key, 13)
    f32 = jnp.float32
    dh = HEAD_DIM
    x = jax.random.normal(ks[0], (BATCH, SEQ, D_MODEL), f32)
    norm_g = 1.0 + 0.05 * jax.random.normal(ks[1], (DEPTH, 4, D_MODEL), f32)
    attn_w_in = jax.random.normal(ks[2], (N_EVEN, D_MODEL, IN_COLS), f32) * D_MODEL ** -0.5
    fox_b_f = jax.random.uniform(ks[3], (N_EVEN, FOX_HEADS), f32, 1.0, 4.0)
    nsa_cmp_pe = 0.1 * jax.random.normal(ks[4], (N_EVEN, 2, CMP_BLOCK, dh), f32)
    nsa_cmp_w1 = jax.random.normal(ks[5], (N_EVEN, 2, CMP_BLOCK * dh, dh), f32) * (CMP_BLOCK * dh) ** -0.5
    nsa_cmp_w2 = jax.random.normal(ks[6], (N_EVEN, 2, dh, dh), f32) * dh ** -0.5
    attn_w_out = jax.random.normal(ks[7], (N_EVEN, MIX_WIDTH, D_MODEL), f32) * MIX_WIDTH ** -0.5
    pool_w = jax.random.normal(ks[8], (N_ODD, len(POOL_WINDOWS), POOL_GROUP, POOL_GROUP), f32) * POOL_GROUP ** -0.5
    pool_scale = 1.0 + 0.1 * jax.random.normal(ks[9], (N_ODD, D_MODEL), f32)
    ffn_w_gate = jax.random.normal(ks[10], (DEPTH, D_MODEL, FFN_HIDDEN), f32) * D_MODEL ** -0.5
    ffn_w_up = jax.random.normal(ks[11], (DEPTH, D_MODEL, FFN_HIDDEN), f32) * D_MODEL ** -0.5
    ffn_w_down = jax.random.normal(ks[12], (DEPTH, FFN_HIDDEN, D_MODEL), f32) * FFN_HIDDEN ** -0.5
    return {"x": x, "norm_g": norm_g, "attn_w_in": attn_w_in, "fox_b_f": fox_b_f,
            "nsa_cmp_pe": nsa_cmp_pe, "nsa_cmp_w1": nsa_cmp_w1, "nsa_cmp_w2": nsa_cmp_w2,
            "attn_w_out": attn_w_out, "pool_w": pool_w, "pool_scale": pool_scale,
            "ffn_w_gate": ffn_w_gate, "ffn_w_up": ffn_w_up, "ffn_w_down": ffn_w_down}


def reference(x, norm_g, attn_w_in, fox_b_f, nsa_cmp_pe, nsa_cmp_w1, nsa_cmp_w2, attn_w_out,
              pool_w, pool_scale, ffn_w_gate, ffn_w_up, ffn_w_down):
    h = x
    for layer in range(DEPTH):
        g = norm_g[layer]
        i = layer // 2
        u = rmsnorm(h, g[0])
        if layer % 2 == 0:
            m = mixer_fox_nsa(u, attn_w_in[i], fox_b_f[i], nsa_cmp_pe[i], nsa_cmp_w1[i], nsa_cmp_w2[i], attn_w_out[i])
        else:
            m = mixer_pool(u, pool_w[i], pool_scale[i])
        h = h + rmsnorm(m, g[1])
        u = rmsnorm(h, g[2])
        h = h + rmsnorm(swiglu(u, ffn_w_gate[layer], ffn_w_up[layer], ffn_w_down[layer]), g[3])
    return h
```

```python
import contextlib
import numpy as np
import ml_dtypes
import concourse.bass as bass
import concourse.mybir as mybir
from concourse.bass_utils import run_bass_kernel_spmd

F32 = mybir.dt.float32
BF16 = mybir.dt.bfloat16
AF = mybir.ActivationFunctionType
ALU = mybir.AluOpType
NEG = -30000.0


class Tok:
    __slots__ = ("sem", "val", "pe")

    def __init__(self, sem, val, pe=False):
        self.sem = sem
        self.val = val
        self.pe = pe


class Buf:
    def __init__(self, ctx, name, dma=False):
        self.name = name
        self.w = None
        self.r = {}
        self.sem = None
        self.cnt = 0
        if dma:
            self.sem = ctx.new_sem("b_" + name)


class Eng:
    def __init__(self, ctx, eng, name, is_pe=False):
        self.ctx = ctx
        self.eng = eng
        self.name = name
        self.is_pe = is_pe
        self.sem = ctx.new_sem("e_" + name)
        self.count = 0
        self.waited = {}

    def wait(self, deps):
        for t in deps:
            if t is None:
                continue
            if self.is_pe and t.sem is self.sem:
                continue
            key = id(t.sem)
            if self.waited.get(key, 0) >= t.val:
                continue
            self.eng.wait_ge(t.sem, t.val)
            self.waited[key] = t.val


class Ctx:
    def __init__(self, nc):
        self.nc = nc
        self.es = contextlib.ExitStack()
        self.nsem = 0
        self.act = Eng(self, nc.scalar, "act")
        self.dve = Eng(self, nc.vector, "dve")
        self.pool = Eng(self, nc.gpsimd, "pool")
        self.pe = Eng(self, nc.tensor, "pe", is_pe=True)
        self.sp = Eng(self, nc.sync, "sp")
        self.n_inst = 0
        self.sbes = self.es
        self.allbufs = []

    def global_barrier(self):
        toks = [Tok(e.sem, e.count) for e in (self.act, self.dve, self.pool, self.pe) if e.count > 0]
        toks += [Tok(b.sem, b.cnt) for b in self.allbufs if b.sem is not None and b.cnt > 0]
        for e in (self.sp, self.act, self.dve, self.pool, self.pe):
            e.wait(toks)

    def new_sem(self, name):
        self.nsem += 1
        return self.es.enter_context(self.nc.semaphore(name))

    def sb(self, name, shape, dt):
        self.nsb = getattr(self, "nsb", 0) + 1
        return self.sbes.enter_context(self.nc.sbuf_tensor(f"sb{self.nsb}_" + name, shape, dt))

    def ps(self, name, shape, dt):
        return self.es.enter_context(self.nc.psum_tensor("ps_" + name, shape, dt))

    def buf(self, name, dma=False):
        b = Buf(self, name, dma)
        self.allbufs.append(b)
        return b

    def _deps(self, reads, writes):
        deps = []
        for b in reads:
            if b.w is not None:
                deps.append(b.w)
        for b in writes:
            if b.w is not None:
                deps.append(b.w)
            deps.extend(b.r.values())
        return deps

    def _commit(self, tok, reads, writes):
        for b in reads:
            b.r[id(tok.sem)] = tok
        for b in writes:
            b.w = tok
            b.r = {}

    def op(self, eng, fn, *a, reads=(), writes=(), **kw):
        eng.wait(self._deps(reads, writes))
        inst = getattr(eng.eng, fn)(*a, **kw)
        eng.count += 1
        inst.then_inc(eng.sem, 1)
        tok = Tok(eng.sem, eng.count)
        self._commit(tok, reads, writes)
        self.n_inst += 1
        return tok

    def dma(self, q, out, in_, dst=None, src=None, reads=(), writes=(), indirect=None, **kw):
        reads = list(reads) + ([src] if src is not None else [])
        writes = list(writes) + ([dst] if dst is not None else [])
        semb = dst if (dst is not None and dst.sem is not None) else src
        assert semb is not None and semb.sem is not None
        q.wait(self._deps(reads, writes))
        semb.cnt += 16
        if indirect is not None:
            q.eng.indirect_dma_start(out=out, out_offset=None, in_=in_, in_offset=indirect, **kw).then_inc(semb.sem, 16)
        else:
            q.eng.dma_start(out=out, in_=in_, **kw).then_inc(semb.sem, 16)
        tok = Tok(semb.sem, semb.cnt)
        self._commit(tok, reads, writes)
        self.n_inst += 1
        return tok


def bf16_np(a):
    return np.asarray(a).astype(ml_dtypes.bfloat16)

NFC = 22


def c_drams(nc, NTOK, NOUT):
    d = {}
    d["x"] = nc.dram_tensor("xt", [NTOK, 1024], F32, kind="ExternalInput").ap()
    d["wout"] = nc.dram_tensor("wout", [128, 8, 1024], F32, kind="ExternalInput").ap()
    d["wg"] = nc.dram_tensor("wg", [2, NFC, 128, 1024], F32, kind="ExternalInput").ap()
    d["wu"] = nc.dram_tensor("wu", [2, NFC, 128, 1024], F32, kind="ExternalInput").ap()
    d["wd"] = nc.dram_tensor("wd", [2, NFC, 128, 1024], F32, kind="ExternalInput").ap()
    d["pw"] = nc.dram_tensor("poolw", [128, 8, 256], F32, kind="ExternalInput").ap()
    d["gv"] = nc.dram_tensor("gvec", [8, 1024], F32, kind="ExternalInput").ap()
    d["it"] = nc.dram_tensor("invtab", [4, 16], F32, kind="ExternalInput").ap()
    d["out"] = nc.dram_tensor("out", [NOUT, 1024], F32, kind="ExternalOutput").ap()
    d["s_wg"] = nc.dram_tensor("wg_s", [2, NFC, 128, 1024], BF16, kind="Internal").ap()
    d["s_wu"] = nc.dram_tensor("wu_s", [2, NFC, 128, 1024], BF16, kind="Internal").ap()
    d["s_wd"] = nc.dram_tensor("wd_s", [2, NFC, 128, 1024], BF16, kind="Internal").ap()
    d["s_wout"] = nc.dram_tensor("wout_s", [128, 8, 1024], BF16, kind="Internal").ap()
    d["s_pw"] = nc.dram_tensor("poolw_s", [128, 8, 256], BF16, kind="Internal").ap()
    return d


def emit_prepass(nc, cx, D):
    pool = cx.pool
    NS = 2
    stage = [cx.sb(f"stage{i}", [128, 1024], F32) for i in range(NS)]
    stageb = [cx.sb(f"stageb{i}", [128, 1024], BF16) for i in range(NS)]
    b_stage = [cx.buf(f"stage{i}", True) for i in range(NS)]
    b_stageb = [cx.buf(f"stageb{i}", True) for i in range(NS)]
    b_scr = cx.buf("wscr")
    k = [0]

    def cast_slab(src_ap, dst_dram_ap):
        i = k[0] % NS
        cx.dma(pool, stage[i][:], src_ap, dst=b_stage[i])
        cx.op(pool, "tensor_copy", stageb[i][:], stage[i][:], reads=[b_stage[i]], writes=[b_stageb[i]])
        cx.dma(pool, dst_dram_ap, stageb[i][:], src=b_stageb[i], writes=[b_scr])
        k[0] += 1

    for c in range(8):
        cast_slab(D["wout"][:, c, :], D["s_wout"][:, c, :])
    for hh in range(2):
        cast_slab(D["pw"][:, 4 * hh:4 * hh + 4, :].rearrange("p a b -> p (a b)"),
                  D["s_pw"][:, 4 * hh:4 * hh + 4, :].rearrange("p a b -> p (a b)"))
    for l in range(2):
        for fc in range(NFC):
            cast_slab(D["wg"][l, fc], D["s_wg"][l, fc])
            cast_slab(D["wu"][l, fc], D["s_wu"][l, fc])
            cast_slab(D["wd"][l, fc], D["s_wd"][l, fc])


def emit_C(nc, cx, groups, n_skip_tiles, pbank, b_bank_, pbT, b_bankT_, oT_loader, D):
    NT = sum(groups)
    d_x, d_gv, d_it, d_out = D["x"], D["gv"], D["it"], D["out"]
    s_wg, s_wu, s_wd = D["s_wg"], D["s_wu"], D["s_wd"]
    c_es = contextlib.ExitStack()
    cx.sbes = c_es
    if True:
        act, dve, pool, pe, sp = cx.act, cx.dve, cx.pool, cx.pe, cx.sp
        NMAX = max(groups) * 128
        RING = 2
        wout = cx.sb("wout", [128, 8, 1024], BF16)
        poolw = cx.sb("poolw", [128, 8, 256], BF16)
        gv = cx.sb("gv", [128, 8, 1024], F32)
        invt = cx.sb("invt", [128, 4, 16], F32)
        ident = cx.sb("ident", [128, 128], BF16)
        identf = cx.sb("identf", [128, 128], F32)
        pm = [pbank[0], pbank[1]]
        pg = [pbank[2], pbank[3]]
        pu = [pbank[4], pbank[5]]
        pT = pbT[:].rearrange("p (c k) -> p c k", c=8)
        pTf = pbank[6][:].rearrange("p (c k) -> p c k", c=4)
        B = lambda n, dma=False: cx.buf(n, dma)
        b_wout, b_poolw, b_gv, b_invt, b_id = B("wout", True), B("poolw", True), B("gv", True), B("invt", True), B("id")
        b_oTg, b_hg = B("oTg", True), [B(f"hg{i}", True) for i in range(max(groups))]
        b_tmp, b_ub, b_uf, b_uTg, b_aTg = B("tmp"), B("ub"), B("uf"), B("uTg"), B("aTg")
        b_junk = b_ub
        b_sg = [B("sg0"), B("sg1")]
        b_wgr = [B(f"wgr{i}", True) for i in range(RING)]
        b_wdall = B("wdall", True)
        b_ss = B("ss")
        b_u1T, b_pa, b_pb, b_plT = B("u1T"), B("pa"), B("pb"), b_oTg
        b_pg, b_pu = [B("pg0"), B("pg1")], [B("pu0"), B("pu1")]
        b_pmA = B("pmA")
        pmsets = [(pm, b_pmA), (pg, None)]
        b_pT, b_pTf = B("pT"), B("pTf")
        b_scr = B("scr")

        cx.op(pool, "memset", identf[:], 0.0, writes=[b_id])
        cx.op(pool, "affine_select", identf[:], identf[:], [[-1, 128]], ALU.not_equal, 1.0,
              base=0, channel_multiplier=1, reads=[b_id], writes=[b_id])
        cx.op(pool, "tensor_copy", ident[:], identf[:], reads=[b_id], writes=[b_id])
        for r in range(8):
            cx.dma(sp, gv[:, r, :], d_gv[r].partition_broadcast(128), dst=b_gv)
        for r in range(4):
            cx.dma(sp, invt[:, r, :], d_it[r].partition_broadcast(128), dst=b_invt)
        cx.dma(sp, wout[:], D["s_wout"], dst=b_wout)
        cx.dma(sp, poolw[:], D["s_pw"], dst=b_poolw)
        hg = cx.sb("hg", [128, max(groups), 1024], F32)
        tmp = cx.sb("tmp", [128, 1024], F32)
        ub = cx.sb("ub", [128, 1024], BF16)
        junk = ub
        uf = cx.sb("uf", [128, 1024], F32)
        uTg = cx.sb("uTg", [128, 8, NMAX], BF16)
        aTg = cx.sb("aTg", [128, NFC, NMAX], BF16)
        sg = [cx.sb(f"sg{i}", [128, NMAX], F32) for i in range(2)]
        wgr = [cx.sb(f"wgr{i}", [128, 1024], BF16) for i in range(RING)]
        wur = [cx.sb(f"wur{i}", [128, 1024], BF16) for i in range(RING)]
        wdall = cx.sb("wdall", [128, NFC, 1024], BF16)
        ss = cx.sb("ss", [128, 8], F32)
        u1T = cx.sb("u1T", [128, 8, 16 + NMAX], F32)
        pa = cx.sb("pa", [128, 16 + NMAX], F32)
        pb = cx.sb("pb", [128, 16 + NMAX], F32)
        oTg = cx.sb("oTg", [128, 8, NMAX], BF16)
        plT = oTg
        cx.op(pool, "memset", u1T[:], 0.0, writes=[b_u1T])
        GI = {"g01": 0, "g02": 1, "g03": 2, "g10": 3, "g11": 4, "g12": 5, "g13": 6, "psc": 7}
        nss = [0]

        def rms_scale(src_ap, src_buf, gname, out_ap, out_buf, extra_reads=()):
            j = nss[0] % 8
            nss[0] += 1
            cx.op(act, "activation", junk[:], src_ap, AF.Square, accum_out=ss[:, j:j + 1],
                  reads=[src_buf], writes=[b_junk, b_ss])
            cx.op(act, "activation", ss[:, j:j + 1], ss[:, j:j + 1], AF.Sqrt, bias=1e-6, scale=1.0 / 1024,
                  reads=[b_ss], writes=[b_ss])
            cx.op(dve, "reciprocal", ss[:, j:j + 1], ss[:, j:j + 1], reads=[b_ss], writes=[b_ss])
            cx.op(dve, "scalar_tensor_tensor", out_ap, src_ap, ss[:, j:j + 1], gv[:, GI[gname], :], ALU.mult, ALU.mult,
                  reads=[src_buf, b_ss, b_gv] + list(extra_reads), writes=[out_buf])

        def transpose_bf(src_sb, src_buf, dstT, dst_buf, col0):
            for c in range(8):
                cx.op(pe, "transpose", pT[:, c, :], src_sb[:, c * 128:(c + 1) * 128], ident[:],
                      reads=[src_buf, b_id], writes=[b_pT])
            cx.op(act, "copy", dstT[:, :, col0:col0 + 128], pT, reads=[b_pT], writes=[dst_buf])

        def ffn(layer, T, gname_post):
            N = T * 128
            cx.dma(sp, wdall[:], s_wd[layer].rearrange("f p n -> p f n"), dst=b_wdall, reads=[b_scr])
            for fc in range(NFC):
                r = fc % RING
                cx.dma(sp, wgr[r][:], s_wg[layer, fc], dst=b_wgr[r], reads=[b_scr])
                cx.dma(sp, wur[r][:], s_wu[layer, fc], dst=b_wgr[r], reads=[b_scr])
                pb_ = fc % 2
                for c in range(8):
                    cx.op(pe, "matmul", pg[pb_][:, :N], wgr[r][:, c * 128:(c + 1) * 128], uTg[:, c, :N],
                          start=(c == 0), stop=(c == 7), reads=[b_wgr[r], b_uTg], writes=[b_pg[pb_]])
                for c in range(8):
                    cx.op(pe, "matmul", pu[pb_][:, :N], wur[r][:, c * 128:(c + 1) * 128], uTg[:, c, :N],
                          start=(c == 0), stop=(c == 7), reads=[b_wgr[r], b_uTg], writes=[b_pu[pb_]])
                cx.op(act, "activation", sg[pb_][:, :N], pg[pb_][:, :N], AF.Silu, reads=[b_pg[pb_]], writes=[b_sg[pb_]])
                cx.op(dve, "tensor_tensor", aTg[:, fc, :N], sg[pb_][:, :N], pu[pb_][:, :N], ALU.mult,
                      reads=[b_sg[pb_], b_pu[pb_]], writes=[b_aTg])
            def down(t):
                pmt, wr = pmset(t)
                for hf in range(2):
                    for fc in range(NFC):
                        cx.op(pe, "matmul", pmt[hf][:], aTg[:, fc, t * 128:(t + 1) * 128], wdall[:, fc, hf * 512:(hf + 1) * 512],
                              start=(fc == 0), stop=(fc == NFC - 1), reads=[b_aTg, b_wdall], writes=wr[hf])
            down(0)
            for t in range(T):
                if t + 1 < T:
                    down(t + 1)
                post_norm_add(t, gname_post)

        def pmset(t):
            if t % 2 == 0:
                return pm, [[b_pmA], [b_pmA]]
            return pg, [[b_pg[0]], [b_pg[1]]]

        def post_norm_add(t, gname):
            pm, wr_ = pmset(t)
            b_pm0, b_pm1 = wr_[0][0], wr_[1][0]
            j = nss[0] % 8
            nss[0] += 1
            cx.op(act, "activation", junk[:, 0:512], pm[0][:], AF.Square, accum_out=ss[:, j:j + 1],
                  reads=[b_pm0], writes=[b_junk, b_ss])
            j2 = nss[0] % 8
            nss[0] += 1
            cx.op(act, "activation", junk[:, 512:1024], pm[1][:], AF.Square, accum_out=ss[:, j2:j2 + 1],
                  reads=[b_pm1], writes=[b_junk, b_ss])
            cx.op(dve, "tensor_tensor", ss[:, j:j + 1], ss[:, j:j + 1], ss[:, j2:j2 + 1], ALU.add, reads=[b_ss], writes=[b_ss])
            cx.op(act, "activation", ss[:, j:j + 1], ss[:, j:j + 1], AF.Sqrt, bias=1e-6, scale=1.0 / 1024,
                  reads=[b_ss], writes=[b_ss])
            cx.op(dve, "reciprocal", ss[:, j:j + 1], ss[:, j:j + 1], reads=[b_ss], writes=[b_ss])
            for hf in range(2):
                cx.op(dve, "scalar_tensor_tensor", tmp[:, hf * 512:(hf + 1) * 512], pm[hf][:], ss[:, j:j + 1],
                      gv[:, GI[gname], hf * 512:(hf + 1) * 512], ALU.mult, ALU.mult,
                      reads=[wr_[hf][0], b_ss, b_gv], writes=[b_tmp])
            cx.op(pool, "tensor_tensor", hg[:, t, :], hg[:, t, :], tmp[:], ALU.add, reads=[b_tmp], writes=[b_hg[t]])

        tile0 = 0
        for gi_, T in enumerate(groups):
            N = T * 128
            tok0 = tile0 * 128
            oT_loader(cx, oTg, b_oTg, tok0, N)
            for t in range(T):
                cx.dma(sp, hg[:, t, :], d_x[tok0 + t * 128: tok0 + (t + 1) * 128, :], dst=b_hg[t])
            def mproj(t):
                pmt, wr = pmset(t)
                for hf in range(2):
                    for c in range(8):
                        cx.op(pe, "matmul", pmt[hf][:], oTg[:, c, t * 128:(t + 1) * 128], wout[:, c, hf * 512:(hf + 1) * 512],
                              start=(c == 0), stop=(c == 7), reads=[b_oTg, b_wout], writes=wr[hf])
            mproj(0)
            for t in range(T):
                post_norm_add(t, "g01")
                rms_scale(hg[:, t, :], b_hg[t], "g02", ub[:], b_ub)
                if t + 1 < T:
                    mproj(t + 1)
                transpose_bf(ub, b_ub, uTg, b_uTg, t * 128)
            ffn(0, T, "g03")
            for t in range(T):
                rms_scale(hg[:, t, :], b_hg[t], "g10", uf[:], b_uf)
                for q4 in range(2):
                    for c in range(4):
                        cc = q4 * 4 + c
                        cx.op(pe, "transpose", pTf[:, c, :], uf[:, cc * 128:(cc + 1) * 128], identf[:],
                              reads=[b_uf, b_id], writes=[b_pTf])
                    cx.op(act, "copy", u1T[:, q4 * 4:q4 * 4 + 4, 16 + t * 128:16 + (t + 1) * 128], pTf,
                          reads=[b_pTf], writes=[b_u1T])
            halo_only = bool(n_skip_tiles) and gi_ == 0
            if not halo_only:
                W = 16 + N
                for c in range(8):
                    lw = c // 2 + 1
                    w = 1 << lw
                    src = u1T[:, c, :W]
                    srcb = b_u1T
                    bufs = [(pa, b_pa), (pb, b_pb)]
                    for lv in range(lw):
                        sh = 1 << lv
                        dst, dstb = bufs[lv % 2]
                        cx.op(dve, "tensor_tensor", dst[:, sh:W], src[:, sh:W], src[:, 0:W - sh], ALU.add,
                              reads=[srcb], writes=[dstb])
                        src, srcb = dst[:, :W], dstb
                    cx.op(dve, "scalar_tensor_tensor", plT[:, c, :N], src[:, 16:W], 1.0 / w, u1T[:, c, 16:W],
                          ALU.mult, ALU.subtract, reads=[srcb, b_u1T], writes=[b_plT])
                    if gi_ == (1 if n_skip_tiles else 0):
                        cx.op(dve, "tensor_tensor", tmp[:, 0:16], src[:, 16:32], invt[:, c // 2, :], ALU.mult,
                              reads=[srcb, b_invt], writes=[b_tmp])
                        cx.op(dve, "tensor_tensor", plT[:, c, 0:16], tmp[:, 0:16], u1T[:, c, 16:32], ALU.subtract,
                              reads=[b_tmp, b_u1T], writes=[b_plT])
            cx.op(pool, "tensor_copy", pa[:, 0:128].rearrange("p (c k) -> p c k", c=8), u1T[:, :, N:N + 16],
                  reads=[b_u1T], writes=[b_pa])
            cx.op(pool, "tensor_copy", u1T[:, :, 0:16], pa[:, 0:128].rearrange("p (c k) -> p c k", c=8),
                  reads=[b_pa], writes=[b_u1T])
            if not halo_only:
                for t in range(T):
                    for g4 in range(4):
                        for cc in range(2):
                            hf = g4 // 2
                            cx.op(pe, "matmul", pm[hf][:, (g4 % 2) * 256:(g4 % 2) * 256 + 256],
                                  plT[:, 2 * g4 + cc, t * 128:(t + 1) * 128], poolw[:, 2 * g4 + cc, :],
                                  start=(cc == 0), stop=(cc == 1), reads=[b_plT, b_poolw], writes=[b_pmA])
                    for hf in range(2):
                        cx.op(dve, "tensor_tensor", uf[:, hf * 512:(hf + 1) * 512], pm[hf][:], gv[:, GI["psc"], hf * 512:(hf + 1) * 512],
                              ALU.mult, reads=[b_pmA, b_gv], writes=[b_uf])
                    rms_scale(uf[:], b_uf, "g11", tmp[:], b_tmp)
                    cx.op(pool, "tensor_tensor", hg[:, t, :], hg[:, t, :], tmp[:], ALU.add, reads=[b_tmp], writes=[b_hg[t]])
                    rms_scale(hg[:, t, :], b_hg[t], "g12", ub[:], b_ub)
                    transpose_bf(ub, b_ub, uTg, b_uTg, t * 128)
                ffn(1, T, "g13")
            for t in range(T):
                gt = tile0 + t
                if gt >= n_skip_tiles:
                    o0 = (gt - n_skip_tiles) * 128
                    cx.dma(sp, d_out[o0:o0 + 128, :], hg[:, t, :], src=b_hg[t])
            tile0 += T
        cx.global_barrier()
    c_es.close()
    cx.sbes = cx.es


def c_weight_inputs(inp):
    d = {}
    d["wout"] = np.ascontiguousarray(inp["attn_w_out"][0].reshape(8, 128, 1024).transpose(1, 0, 2))
    wg = inp["ffn_w_gate"].reshape(2, 8, 128, NFC, 128)
    d["wg"] = np.ascontiguousarray(wg.transpose(0, 3, 2, 1, 4).reshape(2, NFC, 128, 1024))
    wu = inp["ffn_w_up"].reshape(2, 8, 128, NFC, 128)
    d["wu"] = np.ascontiguousarray(wu.transpose(0, 3, 2, 1, 4).reshape(2, NFC, 128, 1024))
    d["wd"] = np.ascontiguousarray(inp["ffn_w_down"].reshape(2, NFC, 128, 1024))
    pw = inp["pool_w"][0].reshape(4, 2, 128, 256)
    d["poolw"] = np.ascontiguousarray(pw.transpose(2, 0, 1, 3).reshape(128, 8, 256))
    ng = inp["norm_g"]
    d["gvec"] = np.ascontiguousarray(np.stack([ng[0, 1], ng[0, 2], ng[0, 3], ng[1, 0], ng[1, 1], ng[1, 2], ng[1, 3],
                                               inp["pool_scale"][0]]).astype(np.float32))
    return d


def inv_table(seq_start):
    t = np.zeros((4, 16), np.float32)
    for gi, w in enumerate((2, 4, 8, 16)):
        for i in range(16):
            t[gi, i] = 1.0 / (min(i + 1, w) if seq_start else w)
    return t

BIGV = 1e30
NEGINF = -1e30
NFM = 776
SLOPES = [2.0 ** (-(i + 1)) for i in range(8)]


def emit_AB(nc, cx, S, pbank, b_bank, pbT, b_bankT, out_writer, cc_hook=None):
    NTL = S // 128
    NCH = S // 512
    NC = S // 16 - 1
    NCT = (NC + 127) // 128
    NCP = NCT * 128
    NB = S // 64
    NJT = (NB + 127) // 128
    DI = lambda n, sh, dt: nc.dram_tensor(n, sh, dt, kind="ExternalInput").ap()
    DS = lambda n, sh, dt: nc.dram_tensor(n, sh, dt, kind="Internal").ap()
    d_x = DI("x", [S, 1024], F32)
    d_g0 = DI("g0", [1024], F32)
    d_wfm = DI("wfm", [128, 8, NFM], F32)
    d_wtm = DI("wtm", [128, 8, 256], F32)
    d_nbf = DI("bf", [2, 1], F32)
    d_pe = DI("cpe", [2, 64, 32], F32)
    d_w1 = DI("cw1", [2, 64, 32 * 64], F32)
    d_w2 = DI("cw2", [2, 64, 64], F32)
    d_alq = DI("alibiQ", [4, 4, S], BF16)
    d_posk = DI("posK", [4, S], BF16)
    d_blkk = DI("blkK", [2, S], BF16)
    d_cmpk = DI("cmpK", [4, NCP], BF16)
    d_tri = DI("tri", [128, 2, 128], BF16)
    d_id = DI("identb", [128, 128], BF16)
    d_idf = DI("identf", [128, 128], F32)
    d_lst = DI("lstrict", [128, 128], F32)
    d_cmask = DI("cmpmask", [128, 5, 512], BF16)
    d_spatch = DI("selpatch", [128, 10], BF16)
    d_k3 = DI("keepadd3", [128, 2, 3], F32)
    s_qf = DS("s_qf", [2, 64, S], BF16)
    s_kf = DS("s_kf", [2, 64, S], BF16)
    s_qn = DS("s_qn", [4, 64, S], BF16)
    s_kv = DS("s_kv", [4, 64, S], BF16)
    s_vv = DS("s_vv", [S, 256], BF16)
    s_lf = DS("s_lf", [2, S], F32)
    s_gt = DS("s_gt", [6, S], F32)
    s_cr = DS("s_cr", [2, 6, S], BF16)
    s_selb = DS("s_selb", [NJT, 128, S], BF16)

    ab_es = contextlib.ExitStack()
    cx.sbes = ab_es
    if True:
        act, dve, pool, pe, sp = cx.act, cx.dve, cx.pool, cx.pe, cx.sp
        B = lambda n, dma=False: cx.buf(n, dma)
        ident = cx.sb("ident", [128, 128], BF16)
        identf = cx.sb("identf", [128, 128], F32)
        tri = cx.sb("tri", [128, 2, 128], BF16)
        ones65 = cx.sb("ones65", [128, 64], F32)
        b_const = B("const", True)
        cx.dma(sp, ident[:], d_id, dst=b_const)
        cx.dma(sp, identf[:], d_idf, dst=b_const)
        cx.dma(sp, tri[:], d_tri, dst=b_const)
        b_ones = B("ones65")
        cx.op(dve, "memset", ones65[:], 1.0, writes=[b_ones])
        b_scr = B("scr")

        st1 = contextlib.ExitStack()
        S1 = lambda n, sh, dt: st1.enter_context(nc.sbuf_tensor("s1_" + n, sh, dt))
        wfm = S1("wfm", [128, 8, NFM], BF16)
        wtm = S1("wtm", [128, 8, 256], BF16)
        g0 = S1("g0", [128, 1024], F32)
        nbf = S1("nbf", [2, 1], F32)
        stg = [S1(f"stg{i}", [128, 1024], F32) for i in range(2)]
        xs = [S1(f"xs{i}", [128, 1024], F32) for i in range(8)]
        junk = S1("junk", [128, 1024], BF16)
        ub = [S1(f"ub{i}", [128, 1024], BF16) for i in range(3)]
        ss = S1("ss", [128, 8], F32)
        uTg = [S1(f"uTg{i}", [128, 8, 512], BF16) for i in range(2)]
        ev = [S1(f"ev{i}", [128, 512], BF16) for i in range(4)]
        vsb = [S1(f"vsb{i}", [128, 256], BF16) for i in range(2)]
        sm = [S1(f"sm{i}", [8, 512], F32) for i in range(4)]
        b_wfm, b_wtm, b_g0, b_nbf = B("wfm"), B("wtm"), B("g0", True), B("nbf", True)
        b_stg = [B(f"stg{i}", True) for i in range(2)]
        b_xs = [B(f"xs{i}", True) for i in range(8)]
        b_junk, b_ss = B("junk"), B("ss")
        b_ub = [B("ub0"), B("ub1"), B("ub2")]
        b_uTg = [B("uTg0"), B("uTg1")]
        b_ev = [B(f"ev{i}", True) for i in range(4)]
        b_vsb = [B(f"vsb{i}", True) for i in range(2)]
        b_sm = [B(f"sm{i}", True) for i in range(4)]

        cx.dma(sp, g0[:], d_g0.partition_broadcast(128), dst=b_g0)
        cx.dma(sp, nbf[:], d_nbf, dst=b_nbf)
        cx.op(dve, "tensor_scalar", nbf[:], nbf[:], -1.0, None, ALU.mult, reads=[b_nbf], writes=[b_nbf])
        k = 0
        for c in range(8):
            i = k % 2
            cx.dma(sp, stg[i][:, :NFM], d_wfm[:, c, :], dst=b_stg[i])
            cx.op(dve, "tensor_copy", wfm[:, c, :], stg[i][:, :NFM], reads=[b_stg[i]], writes=[b_wfm])
            k += 1
        for c in range(8):
            i = k % 2
            cx.dma(sp, stg[i][:, :256], d_wtm[:, c, :], dst=b_stg[i])
            cx.op(dve, "tensor_copy", wtm[:, c, :], stg[i][:, :256], reads=[b_stg[i]], writes=[b_wtm])
            k += 1

        pTv = pbT[:]
        fm_groups = [(0, 128, 0.125, s_qf, (0, 1)), (128, 128, 1.0, s_kf, (0, 1)), (256, 128, 0.125, s_qn, (0, 1)),
                     (384, 128, 0.125, s_qn, (2, 3)), (512, 128, 1.0, s_kv, (0, 1)), (640, 128, 1.0, s_kv, (2, 3))]
        nss = 0
        evi = 0
        smi = 0
        pfi = 0
        NTT = NCH * 4
        pTv2 = [pbT[:], pbank[0][:].bitcast(BF16)]
        b_pTv2 = [b_bankT, b_bank[0]]
        pV2 = [pbank[1], pbank[6]]
        b_pV2 = [b_bank[1], b_bank[6]]
        ss_of = {}

        def stA(tile_i):
            nonlocal nss
            xi = tile_i % 8
            j = nss % 8
            nss += 1
            cx.op(act, "activation", junk[:], xs[xi][:], AF.Square, accum_out=ss[:, j:j + 1],
                  reads=[b_xs[xi]], writes=[b_junk, b_ss])
            cx.op(act, "activation", ss[:, j:j + 1], ss[:, j:j + 1], AF.Sqrt, bias=1e-6, scale=1.0 / 1024,
                  reads=[b_ss], writes=[b_ss])
            cx.op(dve, "reciprocal", ss[:, j:j + 1], ss[:, j:j + 1], reads=[b_ss], writes=[b_ss])
            u_, b_u = ub[tile_i % 3], b_ub[tile_i % 3]
            cx.op(dve, "scalar_tensor_tensor", u_[:], xs[xi][:], ss[:, j:j + 1], g0[:], ALU.mult, ALU.mult,
                  reads=[b_xs[xi], b_ss, b_g0], writes=[b_u])

        def stB(tile_i):
            gch, t = tile_i // 4, tile_i % 4
            ug, b_ug = uTg[gch % 2], b_uTg[gch % 2]
            u_, b_u = ub[tile_i % 3], b_ub[tile_i % 3]
            pv_, b_pv = pTv2[tile_i % 2], b_pTv2[tile_i % 2]
            for c in range(8):
                cx.op(pe, "transpose", pv_[:, c * 128:(c + 1) * 128], u_[:, c * 128:(c + 1) * 128], ident[:],
                      reads=[b_u, b_const], writes=[b_pv])
            cx.op(act, "copy", ug[:, :, t * 128:(t + 1) * 128], pv_.rearrange("p (c k) -> p c k", c=8),
                  reads=[b_pv], writes=[b_ug])
            pvv, b_pvv = pV2[tile_i % 2], b_pV2[tile_i % 2]
            for c in range(8):
                cx.op(pe, "matmul", pvv[:, 0:256], ug[:, c, t * 128:(t + 1) * 128], wtm[:, c, :],
                      start=(c == 0), stop=(c == 7), reads=[b_ug, b_wtm], writes=[b_pvv])
            vi = tile_i % 2
            cx.op(dve, "tensor_copy", vsb[vi][:], pvv[:, 0:256], reads=[b_pvv], writes=[b_vsb[vi]])
            cx.dma(sp, s_vv[tile_i * 128:(tile_i + 1) * 128, :], vsb[vi][:], src=b_vsb[vi], writes=[b_scr])

        def stC(gch):
            nonlocal evi, smi, pfi
            ug, b_ug = uTg[gch % 2], b_uTg[gch % 2]
            cols = slice(gch * 512, (gch + 1) * 512)
            for (c0, M, scl, dst, idx) in fm_groups:
                bk = 2 + pfi % 4
                pfi += 1
                for c in range(8):
                    cx.op(pe, "matmul", pbank[bk][0:M, :], wfm[:, c, c0:c0 + M], ug[:, c, :],
                          start=(c == 0), stop=(c == 7), reads=[b_wfm, b_ug], writes=[b_bank[bk]])
                e_ = evi % 4
                evi += 1
                if evi % 2:
                    cx.op(act, "activation", ev[e_][:], pbank[bk][:], AF.Copy, scale=scl, reads=[b_bank[bk]], writes=[b_ev[e_]])
                else:
                    cx.op(dve, "tensor_scalar", ev[e_][:], pbank[bk][:], scl, None, ALU.mult, reads=[b_bank[bk]], writes=[b_ev[e_]])
                cx.dma(sp, dst[idx[0], :, cols], ev[e_][0:64, :], src=b_ev[e_], writes=[b_scr])
                cx.dma(sp, dst[idx[1], :, cols], ev[e_][64:128, :], src=b_ev[e_], writes=[b_scr])
            bk = 2 + pfi % 4
            pfi += 1
            for c in range(8):
                cx.op(pe, "matmul", pbank[bk][0:2, :], wfm[:, c, 768:770], ug[:, c, :],
                      start=(c == 0), stop=(c == 7), reads=[b_wfm, b_ug], writes=[b_bank[bk]])
            s_ = smi % 4
            smi += 1
            cx.op(act, "activation", sm[s_][0:2, :], pbank[bk][0:2, :], AF.Exp, bias=nbf[:], scale=-1.0,
                  reads=[b_bank[bk], b_nbf], writes=[b_sm[s_]])
            cx.op(act, "activation", sm[s_][0:2, :], sm[s_][0:2, :], AF.Ln, bias=1.0, scale=1.0,
                  reads=[b_sm[s_]], writes=[b_sm[s_]])
            cx.op(dve, "tensor_scalar", sm[s_][0:2, :], sm[s_][0:2, :], -1.0, None, ALU.mult, reads=[b_sm[s_]], writes=[b_sm[s_]])
            cx.dma(sp, s_lf[:, cols], sm[s_][0:2, :], src=b_sm[s_], writes=[b_scr])
            bk = 2 + pfi % 4
            pfi += 1
            for c in range(8):
                cx.op(pe, "matmul", pbank[bk][0:6, :], wfm[:, c, 770:776], ug[:, c, :],
                      start=(c == 0), stop=(c == 7), reads=[b_wfm, b_ug], writes=[b_bank[bk]])
            s_ = smi % 4
            smi += 1
            cx.op(act, "activation", sm[s_][0:6, :], pbank[bk][0:6, :], AF.Sigmoid, reads=[b_bank[bk]], writes=[b_sm[s_]])
            cx.dma(sp, s_gt[:, cols], sm[s_][0:6, :], src=b_sm[s_], writes=[b_scr])

        SKEW = 2
        PF = 5

        def stX(tile_i):
            xi = tile_i % 8
            cx.dma(sp, xs[xi][:], d_x[tile_i * 128:(tile_i + 1) * 128, :], dst=b_xs[xi])

        for i in range(min(PF, NTT)):
            stX(i)
        for i in range(NTT + SKEW):
            if i + PF < NTT:
                stX(i + PF)
            if i < NTT:
                stA(i)
            jt_ = i - SKEW
            if 0 <= jt_ < NTT:
                stB(jt_)
                if jt_ % 4 == 3:
                    stC(jt_ // 4)

        def full_barrier(bufs):
            fin = [t for b in bufs for t in list(b.r.values()) + ([b.w] if b.w is not None else [])]
            for e_ in (sp, act, dve, pool, pe):
                e_.wait(fin)

        full_barrier(b_ev + b_vsb + b_sm + b_bank + b_uTg + b_ub + [b_junk, b_ss, b_bankT])
        st1.close()

        KA = cx.sb("KA", [128, S], BF16)
        KBb = cx.sb("KB", [128, S], BF16)
        VA = cx.sb("VA", [128, NTL, 65], BF16)
        VB = cx.sb("VB", [128, NTL, 65], BF16)
        b_KA, b_KB, b_VA, b_VB = B("KA", True), B("KB", True), B("VA", True), B("VB", True)
        QC = [cx.sb(f"QC{i}", [128, 512], BF16) for i in range(3)]
        b_QC = [B(f"QC{i}", True) for i in range(3)]
        PT = [cx.sb(f"PT{i}", [128, 512], BF16) for i in range(4)]
        b_PT = [B(f"PT{i}") for i in range(4)]
        osb = [cx.sb(f"osb{i}", [128, 512], F32) for i in range(3)]
        b_osb = [B("osb0"), B("osb1"), B("osb2")]
        wrow = [cx.sb(f"wrow{i}", [128, 512], F32) for i in range(3)]
        b_wrow = [B("wrow0"), B("wrow1"), B("wrow2")]
        obf = [cx.sb(f"obf{i}", [64, 512], BF16) for i in range(2)]
        b_obf = [B("obf0", True), B("obf1", True)]
        onsa = cx.sb("onsa", [64, 512], F32)
        otmp = cx.sb("otmp", [64, 512], F32)
        b_onsa, b_otmp = B("onsa"), B("otmp")
        gsb = [cx.sb(f"gsb{i}", [128, 3, 512], F32) for i in range(2)]
        b_gsb = [B("gsb0", True), B("gsb1", True)]
        PS_S, PS_O, PS_B = [0, 1, 2], [3, 4], 5

        def load_K(dstT, dstB, src_ap, aug_fn):
            cx.dma(sp, dstT[0:64, :], src_ap, dst=dstB)
            aug_fn(dstT, dstB)

        def load_V(dstV, dstB, col0):
            cx.op(pool, "memset", dstV[:, :, 64:65], 1.0, writes=[dstB])
            for h0 in range(0, NTL, 16):
                h1 = min(NTL, h0 + 16)
                cx.dma(sp, dstV[:, h0:h1, 0:64],
                       s_vv[h0 * 128:h1 * 128, col0:col0 + 64].rearrange("(t k) d -> k t d", k=128), dst=dstB, reads=[b_scr])

        class Pipe:
            def __init__(self, sbanks=None):
                self.sb = sbanks if sbanks is not None else PS_S
                self.ns = len(self.sb)
                self.items = []
                self.n = 0
                self.deferred = []

            def add(self, it):
                i = self.n
                self.n += 1
                it["i"] = i
                sb_, pt_ = self.sb[i % self.ns], i % self.ns
                q0, q1 = it["q0"], it["q1"]
                ex = it.get("extras", [])
                K = it["K"]
                cx.op(pe, "matmul", pbank[sb_][:, q0:q1], it["kT"], it["qT"][0:K, q0:q1], start=True, stop=(len(ex) == 0),
                      reads=it["reads"], writes=[b_bank[sb_]])
                for n_, (l_, r_, a_, b_, rd_) in enumerate(ex):
                    cx.op(pe, "matmul", pbank[sb_][:, a_:b_], l_, r_, start=False, stop=(n_ == len(ex) - 1),
                          reads=rd_, writes=[b_bank[sb_]])
                cx.op(act, "activation", PT[pt_][:, q0:q1], pbank[sb_][:, q0:q1], AF.Exp,
                      reads=[b_bank[sb_]], writes=[b_PT[pt_]])
                self.items.append(it)
                if len(self.items) > self.ns - 1:
                    self._pv(self.items.pop(0))

            def _pv(self, it):
                i = it["i"]
                pt_ = i % self.ns
                q0, q1 = it["q0"], it["q1"]
                ob = it["obank"]
                cx.op(pe, "matmul", pbank[ob][0:65, q0:q1], it["v"], PT[pt_][:, q0:q1], start=it["first"], stop=it["last"],
                      reads=[b_PT[pt_]] + it["vreads"], writes=[b_bank[ob]])
                nd = []
                for (cnt, fn) in self.deferred:
                    if cnt <= 0:
                        fn()
                    else:
                        nd.append((cnt - 1, fn))
                self.deferred = nd
                if it["last"] and it.get("epi") is not None:
                    fnb = it["epi"]()
                    if fnb is not None:
                        self.deferred.append((6, fnb))

            def flush(self):
                while self.items:
                    self._pv(self.items.pop(0))
                for (_, fn) in self.deferred:
                    fn()
                self.deferred = []

        epi_n = [0]

        def make_epilogue(ob, gate_ap, gate_buf, mode, out_ap):
            def epi_a():
                e = epi_n[0] % 3
                epi_n[0] += 1
                cx.op(act, "copy", osb[e][0:65, :], pbank[ob][0:65, :], reads=[b_bank[ob]], writes=[b_osb[e]])
                cx.op(dve, "tensor_scalar", wrow[e][64:65, :], osb[e][64:65, :], 1e-30, None, ALU.max,
                      reads=[b_osb[e]], writes=[b_wrow[e]])
                cx.op(dve, "reciprocal", wrow[e][64:65, :], wrow[e][64:65, :], reads=[b_wrow[e]], writes=[b_wrow[e]])
                if gate_ap is not None:
                    cx.op(dve, "tensor_tensor", wrow[e][64:65, :], wrow[e][64:65, :], gate_ap, ALU.mult,
                          reads=[b_wrow[e], gate_buf], writes=[b_wrow[e]])

                def epi_b():
                    cx.op(pe, "matmul", pbank[PS_B][0:64, :], ones65[64:65, 0:64], wrow[e][64:65, :], start=True, stop=True,
                          reads=[b_wrow[e], b_ones], writes=[b_bank[PS_B]])
                    if mode == "fox":
                        o_ = (epi_n[0] + e) % 2
                        cx.op(dve, "tensor_tensor", obf[o_][:], osb[e][0:64, :], pbank[PS_B][0:64, :], ALU.mult,
                              reads=[b_osb[e], b_bank[PS_B]], writes=[b_obf[o_]])
                        out_writer(cx, obf[o_], b_obf[o_], out_ap)
                    elif mode == "first":
                        cx.op(dve, "tensor_tensor", onsa[:], osb[e][0:64, :], pbank[PS_B][0:64, :], ALU.mult,
                              reads=[b_osb[e], b_bank[PS_B]], writes=[b_onsa])
                    else:
                        cx.op(dve, "tensor_tensor", otmp[:], osb[e][0:64, :], pbank[PS_B][0:64, :], ALU.mult,
                              reads=[b_osb[e], b_bank[PS_B]], writes=[b_otmp])
                        if mode == "mid":
                            cx.op(pool, "tensor_tensor", onsa[:], onsa[:], otmp[:], ALU.add, reads=[b_otmp], writes=[b_onsa])
                        else:
                            o_ = e % 2
                            cx.op(dve, "tensor_tensor", obf[o_][:], onsa[:], otmp[:], ALU.add,
                                  reads=[b_otmp, b_onsa], writes=[b_obf[o_]])
                            out_writer(cx, obf[o_], b_obf[o_], out_ap)
                return epi_b
            return epi_a

        qci = [0]

        def load_qc(src_ap, aug_ap, naug, cols):
            i = qci[0] % 3
            qci[0] += 1
            cx.dma(sp, QC[i][0:64, :], src_ap[:, cols], dst=b_QC[i], reads=[b_scr])
            cx.dma(sp, QC[i][64:64 + naug, :], aug_ap[:, cols], dst=b_QC[i], reads=[b_scr])
            return QC[i], b_QC[i]

        with contextlib.ExitStack() as st2:
            S2 = lambda n, sh, dt: st2.enter_context(nc.sbuf_tensor("s2_" + n, sh, dt))
            lst = S2("lst", [128, 128], F32)
            lf = S2("lf", [128, 128], F32)
            onesf = S2("onesf", [128, 128], F32)
            cw = S2("cw", [128, 128], F32)
            off = S2("off", [128, 1], F32)
            r1 = S2("r1", [128, 128], F32)
            crs = S2("crs", [128, 6, 128], BF16)
            b_c2 = B("c2", True)
            b_crs = B("crs", True)
            cx.dma(sp, lst[:], d_lst, dst=b_c2)
            cx.op(pool, "memset", onesf[:], 1.0, writes=[b_c2])
            for h in range(2):
                cx.dma(sp, lf[0:NTL, :], s_lf[h].rearrange("(t k) -> t k", k=128), dst=b_c2, reads=[b_scr])
                cx.op(dve, "tensor_tensor_scan", cw[0:NTL, :], onesf[0:NTL, :], lf[0:NTL, :], 0.0, ALU.mult, ALU.add,
                      reads=[b_c2], writes=[b_c2])
                cx.op(pe, "matmul", pbank[6][0:NTL, 0:1], lst[0:NTL, 0:NTL], cw[0:NTL, 127:128], start=True, stop=True,
                      reads=[b_c2], writes=[b_bank[6]])
                cx.op(act, "copy", off[0:NTL, :], pbank[6][0:NTL, 0:1], reads=[b_bank[6]], writes=[b_c2])
                cx.op(dve, "tensor_scalar", cw[0:NTL, :], cw[0:NTL, :], off[0:NTL, :], None, ALU.add, reads=[b_c2], writes=[b_c2])
                cx.op(dve, "tensor_copy", crs[0:NTL, 0, :], cw[0:NTL, :], reads=[b_c2], writes=[b_crs])
                cx.op(dve, "tensor_tensor", r1[0:NTL, :], cw[0:NTL, :], crs[0:NTL, 0, :], ALU.subtract, reads=[b_c2, b_crs], writes=[b_c2])
                cx.op(dve, "tensor_copy", crs[0:NTL, 1, :], r1[0:NTL, :], reads=[b_c2], writes=[b_crs])
                cx.op(dve, "tensor_tensor", r1[0:NTL, :], r1[0:NTL, :], crs[0:NTL, 1, :], ALU.subtract, reads=[b_c2, b_crs], writes=[b_c2])
                cx.op(dve, "tensor_copy", crs[0:NTL, 2, :], r1[0:NTL, :], reads=[b_c2], writes=[b_crs])
                cx.op(dve, "tensor_scalar", crs[0:NTL, 3:6, :], crs[0:NTL, 0:3, :], -1.0, None, ALU.mult, reads=[b_crs], writes=[b_crs])
                cx.dma(sp, s_cr[h].rearrange("r (t k) -> t r k", k=128), crs[0:NTL, :, :], src=b_crs, writes=[b_scr])
            full_barrier([b_c2, b_crs])

        KCA = cx.sb("KCA", [128, NCP], BF16)
        VCA = cx.sb("VCA", [128, NCT, 65], BF16)
        b_KCA, b_VCA = B("KCA", True), B("VCA")
        with contextlib.ExitStack() as st3:
            S3 = lambda n, sh, dt: st3.enter_context(nc.sbuf_tensor("s3_" + n, sh, dt))
            w1f = S3("w1f", [64, 2048], F32)
            w1b = S3("w1b", [64, 32, 64], BF16)
            pef = S3("pef", [64, 32], F32)
            peb = S3("peb", [64, 32], BF16)
            w2f = S3("w2f", [64, 64], F32)
            w2b = S3("w2b", [64, 64], BF16)
            b1 = S3("b1", [64, 1], F32)
            hx = S3("hx", [64, NCP], F32)
            hy = S3("hy", [64, NCP], F32)
            hb = S3("hb", [64, NCP], BF16)
            b_c3 = B("c3", True)
            cx.op(pool, "memset", KCA[0:64, :], 0.0, writes=[b_KCA])
            cx.op(pool, "memset", VCA[:], 0.0, writes=[b_VCA])
            cx.op(pool, "memset", VCA[:, :, 64:65], 1.0, writes=[b_VCA])
            cx.op(pool, "memset", hb[:], 0.0, writes=[b_c3])
            cx.dma(sp, KCA[64:68, :], d_cmpk, dst=b_KCA)
            for kvi in range(2):
                cx.dma(sp, w1f[:], d_w1[kvi], dst=b_c3)
                cx.dma(sp, pef[:], d_pe[kvi], dst=b_c3)
                cx.dma(sp, w2f[:], d_w2[kvi], dst=b_c3)
                kin, b_kin = (KA, b_KA) if kvi == 0 else (KBb, b_KB)
                cx.dma(sp, kin[0:64, :], s_kv[kvi], dst=b_kin, reads=[b_scr])
                cx.op(dve, "tensor_copy", w1b[:].rearrange("p l j -> p (l j)"), w1f[:], reads=[b_c3], writes=[b_c3])
                cx.op(dve, "tensor_copy", peb[:], pef[:], reads=[b_c3], writes=[b_c3])
                cx.op(dve, "tensor_copy", w2b[:], w2f[:], reads=[b_c3], writes=[b_c3])
                for l in range(32):
                    cx.op(pe, "matmul", pbank[6][0:64, 0:1], w1b[:, l, :], peb[:, l:l + 1], start=(l == 0), stop=(l == 31),
                          reads=[b_c3], writes=[b_bank[6]])
                cx.op(act, "copy", b1[:], pbank[6][0:64, 0:1], reads=[b_bank[6]], writes=[b_c3])
                kin3 = kin[0:64, :].rearrange("p (n s) -> p n s", s=16)
                for n0 in range(0, NC, 512):
                    n1 = min(NC, n0 + 512)
                    for l in range(32):
                        cx.op(pe, "matmul", pbank[5][0:64, 0:n1 - n0], w1b[:, l, :], kin3[:, n0 + l // 16:n1 + l // 16, l % 16],
                              start=(l == 0), stop=(l == 31), reads=[b_c3, b_kin], writes=[b_bank[5]])
                    cx.op(act, "activation", hx[:, n0:n1], pbank[5][0:64, 0:n1 - n0], AF.Identity, bias=b1[:], scale=1.0,
                          reads=[b_bank[5], b_c3], writes=[b_c3])
                cx.op(dve, "tensor_tensor", hy[:, :NC], hx[:, :NC], hx[:, :NC], ALU.mult, reads=[b_c3], writes=[b_c3])
                cx.op(dve, "tensor_scalar", hy[:, :NC], hy[:, :NC], 0.044715, 1.0, ALU.mult, ALU.add, reads=[b_c3], writes=[b_c3])
                cx.op(dve, "tensor_tensor", hy[:, :NC], hy[:, :NC], hx[:, :NC], ALU.mult, reads=[b_c3], writes=[b_c3])
                cx.op(act, "activation", hy[:, :NC], hy[:, :NC], AF.Sigmoid, scale=1.5957691216057308, reads=[b_c3], writes=[b_c3])
                cx.op(dve, "tensor_tensor", hb[:, :NC], hy[:, :NC], hx[:, :NC], ALU.mult, reads=[b_c3], writes=[b_c3])
                if kvi == 0:
                    for n0 in range(0, NC, 512):
                        n1 = min(NC, n0 + 512)
                        cx.op(pe, "matmul", pbank[5][0:64, 0:n1 - n0], w2b[:], hb[:, n0:n1], start=True, stop=True,
                              reads=[b_c3], writes=[b_bank[5]])
                        cx.op(act, "copy", KCA[0:64, n0:n1], pbank[5][0:64, 0:n1 - n0], reads=[b_bank[5]], writes=[b_KCA])
                else:
                    for nt in range(NCT):
                        cx.op(pe, "matmul", pbank[5][:, 0:64], hb[:, nt * 128:(nt + 1) * 128], w2b[:], start=True, stop=True,
                              reads=[b_c3], writes=[b_bank[5]])
                        cx.op(act, "copy", VCA[:, nt, 0:64], pbank[5][:, 0:64], reads=[b_bank[5]], writes=[b_VCA])
            if NC % 128:
                pass
            full_barrier([b_c3, b_KA, b_KB, b_KCA, b_VCA])

        st4 = contextlib.ExitStack()
        if True:
            S4 = lambda n, sh, dt: st4.enter_context(nc.sbuf_tensor("s4_" + n, sh, dt))
            q4 = [S4(f"q4_{i}", [128, 4, 512], BF16) for i in range(2)]
            b_q4 = [B("q4_0", True), B("q4_1", True)]
            esb = [[S4(f"esb{p}_{i}", [128, 1024], F32) for i in range(2)] for p in range(2)]
            b_esb = [[B(f"esb{p}_{i}") for i in range(2)] for p in range(2)]
            lsum = [S4(f"lsum{p}", [128, 8], F32) for p in range(2)]
            IMP = [S4(f"IMP{p}", [128, 1040], F32) for p in range(2)]
            PSL = [S4(f"PSL{p}", [128, 256], F32) for p in range(2)]
            PS2 = [S4(f"PS2{p}", [128, 256], F32) for p in range(2)]
            m8 = [S4(f"m8{p}", [128, 16], F32) for p in range(2)]
            sbq = [S4(f"sbq{p}", [128, 256], BF16) for p in range(2)]
            selT = [S4(f"selT{i}", [128, 2, 128], BF16) for i in range(2)]
            b_selT = [B("selT0", True), B("selT1", True)]
            spatch = S4("spatch", [128, 10], BF16)
            k3 = S4("k3", [128, 2, 3], F32)
            b_s4 = B("s4", True)
            b_dv = [B("dv0"), B("dv1")]
            b_ls = [B("ls0"), B("ls1")]
            cx.dma(sp, spatch[:], d_spatch, dst=b_s4)
            cx.dma(sp, k3[:], d_k3, dst=b_s4)
            for p in range(2):
                cx.op(dve, "memset", IMP[p][:], 0.0, writes=[b_dv[p]])
                cx.op(dve, "memset", PSL[p][:], NEGINF, writes=[b_dv[p]])
            NBW = min(NB, 256)
            SB_ = [pbank[6][:], pbT[:].bitcast(F32)]
            b_SB = [b_bank[6], b_bankT]
            pTs_all = [pbank[6][:].bitcast(BF16), pbT[:]]
            b_pTs_all = [b_bank[6], b_bankT]

            pending_tail = [None, None]

            def qb_chain(c, sub, qq, b_qq):
                qb = 4 * c + sub
                p = qb % 2
                bv, bl = b_dv[p], b_ls[p]
                if pending_tail[p] is not None:
                    pt_fn = pending_tail[p]
                    pending_tail[p] = None
                    yield from pt_fn()
                NV = min(8 * qb + 7, NC)
                pa_, pb_ = 8 * qb - 2, 8 * qb + 8
                nh = 1 if NV <= 512 else 2
                for r in range(4):
                    eb, b_eb = esb[p][r % 2], b_esb[p][r % 2]
                    for half in range(nh):
                        lo, hi = 512 * half, min(NV, 512 * (half + 1))
                        a_, b_ = max(lo, pa_, 0), min(hi, pb_)
                        has_patch = b_ > a_
                        yield
                        cx.op(pe, "matmul", SB_[p][:, 0:hi - lo], qq[0:68, r, sub * 128:(sub + 1) * 128], KCA[0:68, lo:hi],
                              start=True, stop=not has_patch, reads=[b_qq, b_KCA], writes=[b_SB[p]])
                        if has_patch:
                            cx.op(pe, "matmul", SB_[p][:, a_ - lo:b_ - lo], ident[:], spatch[:, a_ - pa_:b_ - pa_],
                                  start=False, stop=True, reads=[b_const, b_s4], writes=[b_SB[p]])
                        yield
                        cx.op(act, "activation", eb[:, lo:hi], SB_[p][:, 0:hi - lo], AF.Exp,
                              accum_out=lsum[p][:, 2 * r + half:2 * r + half + 1],
                              reads=[b_SB[p]], writes=[b_eb, bl])
                        yield
                    if nh == 2:
                        cx.op(dve, "tensor_tensor", lsum[p][:, 2 * r:2 * r + 1], lsum[p][:, 2 * r:2 * r + 1],
                              lsum[p][:, 2 * r + 1:2 * r + 2], ALU.add, reads=[bl], writes=[bl])
                        yield
                    cx.op(dve, "tensor_scalar", lsum[p][:, 2 * r:2 * r + 1], lsum[p][:, 2 * r:2 * r + 1], 1e-30, None, ALU.max,
                          reads=[bl], writes=[bl])
                    yield
                    cx.op(dve, "reciprocal", lsum[p][:, 2 * r:2 * r + 1], lsum[p][:, 2 * r:2 * r + 1], reads=[bl], writes=[bl])
                    yield
                    if r == 0:
                        cx.op(dve, "tensor_scalar", IMP[p][:, 1:1 + NV], eb[:, 0:NV], lsum[p][:, 0:1], None, ALU.mult,
                              reads=[b_eb, bl], writes=[bv])
                    else:
                        cx.op(dve, "scalar_tensor_tensor", IMP[p][:, 1:1 + NV], eb[:, 0:NV], lsum[p][:, 2 * r:2 * r + 1],
                              IMP[p][:, 1:1 + NV], ALU.mult, ALU.add, reads=[b_eb, bl], writes=[bv])
                    yield
                NJ = min(2 * qb + 2, NBW)
                v = lambda o: IMP[p][:, o:o + 4 * NJ].rearrange("p (j f) -> p j f", f=4)[:, :, 0]
                cx.op(dve, "tensor_tensor", PSL[p][:, 0:NJ], v(0), v(1), ALU.add, reads=[bv], writes=[bv])
                yield
                for o in (2, 3, 4):
                    cx.op(dve, "tensor_tensor", PSL[p][:, 0:NJ], PSL[p][:, 0:NJ], v(o), ALU.add, reads=[bv], writes=[bv])
                    yield
                a3, b3 = max(0, 2 * qb - 1), min(2 * qb + 2, NBW)
                o3 = a3 - (2 * qb - 1)
                cx.op(dve, "tensor_tensor", PSL[p][:, a3:b3], PSL[p][:, a3:b3], k3[:, 0, o3:o3 + b3 - a3], ALU.mult,
                      reads=[bv, b_s4], writes=[bv])
                yield
                cx.op(dve, "tensor_tensor", PSL[p][:, a3:b3], PSL[p][:, a3:b3], k3[:, 1, o3:o3 + b3 - a3], ALU.add,
                      reads=[bv, b_s4], writes=[bv])
                yield
                cx.op(dve, "memset", PSL[p][:, 0:1], BIGV, writes=[bv])
                yield
                cx.op(dve, "max", m8[p][:, 0:8], PSL[p][:, 0:NBW], reads=[bv], writes=[bv])
                yield
                cx.op(dve, "match_replace", PS2[p][:, 0:NBW], m8[p][:, 0:8], PSL[p][:, 0:NBW], -3.0e38, reads=[bv], writes=[bv])
                yield
                cx.op(dve, "max", m8[p][:, 8:16], PS2[p][:, 0:NBW], reads=[bv], writes=[bv])
                yield
                cx.op(dve, "tensor_scalar", sbq[p][:, 0:NBW], PSL[p][:, 0:NBW], m8[p][:, 15:16], NEG, ALU.is_lt, ALU.mult,
                      reads=[bv], writes=[bv])
                yield
                st_, b_st = selT[p], b_selT[p]
                pTs, b_pTs = pTs_all[p], b_pTs_all[p]

                def tail_steps():
                    for jt in range(NJT):
                        nj = min(128, NB - 128 * jt)
                        cx.op(pe, "transpose", pTs[0:nj, jt * 128:(jt + 1) * 128], sbq[p][:, jt * 128:jt * 128 + nj], ident[:],
                              reads=[bv, b_const], writes=[b_pTs])
                    yield
                    yield
                    yield
                    for jt in range(NJT):
                        nj = min(128, NB - 128 * jt)
                        cx.op(act, "copy", st_[0:nj, jt, :], pTs[0:nj, jt * 128:(jt + 1) * 128], reads=[b_pTs], writes=[b_st])
                        cx.dma(sp, s_selb[jt, 0:nj, qb * 128:(qb + 1) * 128], st_[0:nj, jt, :], src=b_st, writes=[b_scr])
                    yield
                pending_tail[p] = tail_steps

            def sel_all():
                for c in range(NCH):
                    cols = slice(c * 512, (c + 1) * 512)
                    qq, b_qq = q4[c % 2], b_q4[c % 2]
                    for r in range(4):
                        cx.dma(sp, qq[0:64, r, :], s_qn[r][:, cols], dst=b_qq, reads=[b_scr])
                        cx.dma(sp, qq[64:68, r, :], d_alq[r][:, cols], dst=b_qq)
                    yield
                    for pair in range(2):
                        gens = [qb_chain(c, 2 * pair, qq, b_qq), qb_chain(c, 2 * pair + 1, qq, b_qq)]
                        alive = [True, True]
                        while any(alive):
                            for gi_ in range(2):
                                if alive[gi_]:
                                    try:
                                        next(gens[gi_])
                                        yield
                                    except StopIteration:
                                        alive[gi_] = False
                for p_ in range(2):
                    if pending_tail[p_] is not None:
                        for _ in pending_tail[p_]():
                            yield
                        pending_tail[p_] = None

            sel_gen = sel_all()
            sel_done = [False]

            def sel_advance(n):
                for _ in range(n):
                    if sel_done[0]:
                        return
                    try:
                        next(sel_gen)
                    except StopIteration:
                        sel_done[0] = True


        def fox_unit(h, Kt, b_Kt, Vt, b_Vt):
            def aug(dT, dB):
                cx.op(pool, "memset", dT[64:70, :], 1.0, writes=[dB])
                cx.dma(sp, dT[67:70, :], s_cr[h, 3:6, :], dst=dB, reads=[b_scr])
            load_K(Kt, b_Kt, s_kf[h], aug)
            load_V(Vt, b_Vt, 64 * h)
            pipe = Pipe()
            for c in range(NCH):
                cols = slice(c * 512, (c + 1) * 512)
                i = qci[0] % 3
                qci[0] += 1
                qc, b_qc = QC[i], b_QC[i]
                cx.op(pool, "memset", qc[64:70, :], 1.0, writes=[b_qc])
                cx.dma(sp, qc[0:64, :], s_qf[h][:, cols], dst=b_qc, reads=[b_scr])
                cx.dma(sp, qc[64:67, :], s_cr[h, 0:3, cols], dst=b_qc, reads=[b_scr])
                ob = PS_O[c % 2]
                nt = 4 * c + 4
                for kt in range(nt):
                    d = kt - 4 * c
                    q0 = 128 * d if d >= 0 else 0
                    ex = []
                    if d >= 0:
                        ex.append((ident[:], tri[:, 0, :], q0, q0 + 128, [b_const]))
                    it = dict(K=70, kT=Kt[0:70, kt * 128:(kt + 1) * 128], qT=qc, q0=q0, q1=512, extras=ex,
                              reads=[b_Kt, b_qc], v=Vt[:, kt, :], vreads=[b_Vt], obank=ob, first=(kt == 0), last=(kt == nt - 1))
                    if kt == nt - 1:
                        it["epi"] = make_epilogue(ob, None, None, "fox", (h, c))
                    pipe.add(it)
                    sel_advance(2)
            pipe.flush()

        fox_unit(0, KA, b_KA, VA, b_VA)
        fox_unit(1, KBb, b_KB, VB, b_VB)

        if True:
            sel_advance(10 ** 9)
            full_barrier([b_s4] + b_dv + b_ls + b_selT + b_q4 + b_esb[0] + b_esb[1] + [b_bank[6], b_bankT])

            st4.close()

        WIN = 8
        NQW = 4
        cmask = cx.sb("cmask", [128, 5, 512], BF16)
        QW = [cx.sb(f"QW{i}", [128, WIN, 512], BF16) for i in range(NQW)]
        b_E, b_QW = B("E", True), [B(f"QW{i}", True) for i in range(NQW)]
        cx.dma(sp, cmask[:], d_cmask, dst=b_E)

        def augpos(dT, dB):
            cx.dma(sp, dT[64:68, :], d_posk, dst=dB)

        def augpos_blk(dT, dB):
            cx.dma(sp, dT[64:68, :], d_posk, dst=dB)
            cx.dma(sp, dT[68:70, :], d_blkk, dst=dB)
        load_K(KA, b_KA, s_kv[2], augpos_blk)
        load_K(KBb, b_KB, s_kv[3], augpos)
        load_V(VA, b_VA, 128)
        load_V(VB, b_VB, 192)
        pipe = Pipe([0, 1, 2, 6])
        gsi = 0
        gw = 0
        for c in range(NCH):
            if cc_hook is not None:
                cc_hook(c)
            cols = slice(c * 512, (c + 1) * 512)
            for rr in range(2):
                qc, b_qc = load_qc(s_qn[rr], d_alq[rr], 4, cols)
                gs, b_gs = gsb[gsi % 2], b_gsb[gsi % 2]
                gsi += 1
                cx.dma(sp, gs[64:65, :, :], s_gt[3 * rr:3 * rr + 3, cols].rearrange("(o r) n -> o r n", o=1), dst=b_gs, reads=[b_scr])
                nt_ = 4 * c + 4

                def prep_window(kt0):
                    nonlocal gw
                    bq = gw % NQW
                    gw += 1
                    nw = min(WIN, nt_ - kt0)
                    cx.op(dve, "tensor_copy", QW[bq][0:68, 0:nw, :], qc[0:68, :].unsqueeze(1).broadcast_to([68, nw, 512]),
                          reads=[b_qc], writes=[b_QW[bq]])
                    jt, j0 = (2 * kt0) // 128, (2 * kt0) % 128
                    cx.dma(sp, QW[bq][68:70, 0:nw, :],
                           s_selb[jt, j0:j0 + 2 * nw, cols].rearrange("(k t) n -> t k n", t=2), dst=b_QW[bq], reads=[b_scr])
                    return bq

                wbuf = {}
                for w_ in range(min(3, (nt_ + WIN - 1) // WIN)):
                    wbuf[w_] = prep_window(w_ * WIN)
                a4, b4 = c // 4, c % 4
                ob = PS_O[0]
                tiles = list(range(a4 + 1))
                for n_, nt in enumerate(tiles):
                    ex = []
                    if nt == a4:
                        ex.append((ident[:], cmask[:, b4, :], 0, 512, [b_const, b_E]))
                    elif nt == a4 - 1 and b4 == 0:
                        ex.append((ident[:], cmask[:, 4, :], 0, 512, [b_const, b_E]))
                    it = dict(K=68, kT=KCA[0:68, nt * 128:(nt + 1) * 128], qT=qc, q0=0, q1=512, extras=ex,
                              reads=[b_KCA, b_qc], v=VCA[:, nt, :], vreads=[b_VCA], obank=ob, first=(n_ == 0), last=(n_ == len(tiles) - 1))
                    if it["last"]:
                        it["epi"] = make_epilogue(ob, gs[64:65, 0, :], b_gs, "first", None)
                    pipe.add(it)
                ob = PS_O[1]
                wl = []
                for i in (3, 2, 1, 0):
                    kt = 4 * c - 4 + i
                    if kt >= 0:
                        wl.append((kt, 0, 128 * (i + 1), (128 * i, 128 * i + 128, 1)))
                for i in range(4):
                    wl.append((4 * c + i, 128 * i, 512, (128 * i, 128 * i + 128, 0)))
                for n_, (kt, q0, q1, (ma, mb, mk)) in enumerate(wl):
                    ex = [(ident[:], tri[:, mk, :], ma, mb, [b_const])]
                    it = dict(K=68, kT=KBb[0:68, kt * 128:(kt + 1) * 128], qT=qc, q0=q0, q1=q1, extras=ex,
                              reads=[b_KB, b_qc], v=VB[:, kt, :], vreads=[b_VB], obank=ob, first=(n_ == 0), last=(n_ == len(wl) - 1))
                    if it["last"]:
                        it["epi"] = make_epilogue(ob, gs[64:65, 2, :], b_gs, "mid", None)
                    pipe.add(it)
                ob = PS_O[0]
                for kt in range(nt_):
                    w_, wi_ = kt // WIN, kt % WIN
                    if wi_ == 0 and (w_ + 3) not in wbuf and (w_ + 3) * WIN < nt_:
                        wbuf[w_ + 3] = prep_window((w_ + 3) * WIN)
                    bq = wbuf[w_]
                    d = kt - 4 * c
                    q0 = 128 * d if d >= 0 else 0
                    ex = []
                    if d >= 0:
                        ex.append((ident[:], tri[:, 0, :], q0, q0 + 128, [b_const]))
                    it = dict(K=70, kT=KA[0:70, kt * 128:(kt + 1) * 128], qT=QW[bq][:, wi_, :], q0=q0, q1=512, extras=ex,
                              reads=[b_KA, b_QW[bq]], v=VA[:, kt, :], vreads=[b_VA], obank=ob, first=(kt == 0), last=(kt == nt_ - 1))
                    if it["last"]:
                        it["epi"] = make_epilogue(ob, gs[64:65, 1, :], b_gs, "last", (2 + rr, c))
                    pipe.add(it)
        pipe.flush()
        cx.global_barrier()
    ab_es.close()
    cx.sbes = cx.es

OFF = {"fq": 0, "fk": 512, "fv": 1024, "ff": 1536, "nq": 1544, "kc": 2056, "vc": 2184, "ks": 2312, "vs": 2440,
       "kw": 2568, "vw": 2696, "ng": 2824}


def ab_constants(S):
    NC = S // 16 - 1
    NCP = ((NC + 127) // 128) * 128
    d = {}
    k = np.arange(128)[:, None]
    q = np.arange(128)[None, :]
    tri = np.zeros((128, 2, 128), np.float32)
    tri[:, 0, :] = np.where(k <= q, 0.0, NEG)
    tri[:, 1, :] = np.where(k > q, 0.0, NEG)
    d["tri"] = bf16_np(tri)
    d["identb"] = bf16_np(np.eye(128, dtype=np.float32))
    d["identf"] = np.eye(128, dtype=np.float32)
    d["lstrict"] = (k < q).astype(np.float32)
    i = np.arange(128)[:, None]
    qq = np.arange(512)[None, :]
    cm = np.zeros((128, 5, 512), np.float32)
    for b in range(4):
        cm[:, b, :] = np.where(16 * i + 31 <= 512 * b + qq, 0.0, NEG)
    cm[:, 4, :] = np.where(16 * i + 31 - 2048 <= qq, 0.0, NEG)
    d["cmpmask"] = bf16_np(cm)
    jj = np.arange(10)[None, :]
    d["selpatch"] = bf16_np(np.where(16 * jj - 1 <= i, 0.0, NEG).astype(np.float32))
    k3 = np.zeros((128, 2, 3), np.float32)
    k3[:64, 0, :] = [0, 0, 0]
    k3[:64, 1, :] = [BIGV, BIGV, NEGINF]
    k3[64:, 0, :] = [1, 0, 0]
    k3[64:, 1, :] = [0, BIGV, BIGV]
    d["keepadd3"] = k3
    t = np.arange(S)
    d["blkK"] = bf16_np(np.stack([(t % 128) < 64, (t % 128) >= 64]).astype(np.float32))
    d["posK"] = bf16_np(np.stack([128.0 * (t // 128), 1.0 * (t % 128), np.ones(S), np.ones(S)]).astype(np.float32))
    ce = 16 * np.arange(NCP) + 31
    d["cmpK"] = bf16_np(np.stack([128.0 * (ce // 128), 1.0 * (ce % 128), np.ones(NCP), np.ones(NCP)]).astype(np.float32))
    return d


def ab_core_inputs(inp, b, j, S, consts):
    w = inp["attn_w_in"][0]
    g = j // 2
    own = [2 * j, 2 * j + 1]
    others = [h for h in range(4 * g, 4 * g + 4) if h not in own]
    qn_order = own + others
    cols = []
    sl = lambda base, h: list(range(OFF[base] + 64 * h, OFF[base] + 64 * h + 64))
    cols += sl("fq", own[0]) + sl("fq", own[1]) + sl("fk", own[0]) + sl("fk", own[1])
    for h in qn_order:
        cols += sl("nq", h)
    cols += sl("kc", g) + sl("vc", g) + sl("ks", g) + sl("kw", g)
    cols += [OFF["ff"] + own[0], OFF["ff"] + own[1]]
    for h in own:
        cols += [OFF["ng"] + 3 * h + r for r in range(3)]
    assert len(cols) == NFM
    wfm = w[:, cols].reshape(8, 128, NFM).transpose(1, 0, 2)
    colt = sl("fv", own[0]) + sl("fv", own[1]) + sl("vs", g) + sl("vw", g)
    wtm = w[:, colt].reshape(8, 128, 256).transpose(1, 0, 2)
    d = dict(consts)
    d["x"] = np.ascontiguousarray(inp["x"][b, :S])
    d["g0"] = np.ascontiguousarray(inp["norm_g"][0, 0])
    d["wfm"] = np.ascontiguousarray(wfm)
    d["wtm"] = np.ascontiguousarray(wtm)
    d["bf"] = np.ascontiguousarray(inp["fox_b_f"][0, own].reshape(2, 1))
    pe_ = inp["nsa_cmp_pe"][0]
    d["cpe"] = np.ascontiguousarray(pe_.transpose(0, 2, 1))
    w1 = inp["nsa_cmp_w1"][0].reshape(2, 32, 64, 64)
    d["cw1"] = np.ascontiguousarray(w1.transpose(0, 2, 1, 3).reshape(2, 64, 2048))
    d["cw2"] = np.ascontiguousarray(inp["nsa_cmp_w2"][0])
    t = np.arange(S)
    al = np.zeros((4, 4, S), np.float32)
    for n_, h in enumerate(qn_order):
        s_ = SLOPES[h]
        al[n_, 0] = s_
        al[n_, 1] = s_
        al[n_, 2] = -s_ * 128.0 * (t // 128)
        al[n_, 3] = -s_ * (t % 128)
    d["alibiQ"] = bf16_np(al)
    return d


def ab_slot_features(j):
    return [64 * (2 * j), 64 * (2 * j + 1), 512 + 64 * (2 * j), 512 + 64 * (2 * j + 1)]


U32 = mybir.dt.uint32


def build_fused(S):
    TPC = S // 4
    NPS = TPC // 512
    groups = [1] + [4] * NPS
    NPC = NPS + 1
    pw = [128] + [512] * NPS
    nc = bass.Bass("TRN2", target_bir_lowering=False)
    src_t = [nc.dram_tensor(f"cc_src{k}", [256, 4 * pw[k]], BF16) for k in range(NPC)]
    gat_t = [nc.dram_tensor(f"cc_gat{k}", [1024, 4 * pw[k]], BF16) for k in range(NPC)]
    d_idx = nc.dram_tensor("oidx", [128, 8], U32, kind="ExternalInput").ap()
    cx = Ctx(nc)
    with cx.es:
        sp, pool = cx.sp, cx.pool
        pbank = [cx.ps(f"bank{i}", [128, 512], F32) for i in range(7)]
        b_bank = [cx.buf(f"bank{i}") for i in range(7)]
        pbT = cx.ps("bankT", [128, 1024], BF16)
        b_bankT = cx.buf("bankT")
        zt = cx.sb("zt", [128, 128], BF16)
        idxsb = cx.sb("idxsb", [128, 8], U32)
        b_zt, b_idx = cx.buf("zt", True), cx.buf("idx", True)
        b_cc = cx.buf("ccsrc")
        src3 = [t.ap().rearrange("f (s w) -> f s w", s=4) for t in src_t]
        cx.op(pool, "memset", zt[:], 0.0, writes=[b_zt])
        cx.dma(sp, src3[0][0:128, 0, :], zt[:], src=b_zt, writes=[b_cc])
        cx.dma(sp, src3[0][128:256, 0, :], zt[:], src=b_zt, writes=[b_cc])
        cx.dma(sp, idxsb[:], d_idx, dst=b_idx)

        def out_writer(cx_, obf_t, b_obf_, key):
            slot, c = key
            j, cl = c // NPS, c % NPS
            cx_.dma(sp, src3[1 + cl][slot * 64:(slot + 1) * 64, j, :], obf_t[:], src=b_obf_, writes=[b_cc])
            if cl == NPS - 1 and j + 1 < 4:
                cx_.dma(sp, src3[0][slot * 64:(slot + 1) * 64, j + 1, :], obf_t[:, 384:512], src=b_obf_, writes=[b_cc])

        D = c_drams(nc, sum(groups) * 128, (sum(groups) - 1) * 128)
        emit_prepass(nc, cx, D)
        csem = cx.new_sem("ccsem")
        obf_bufs = []
        issued = []

        def issue_cc(k):
            toks = [Tok(b.sem, b.cnt) for b in obf_bufs if b.cnt > 0] + [Tok(b_zt.sem, b_zt.cnt)]
            pool.wait(toks)
            nc.gpsimd.collective_compute("AllGather", ALU.bypass, replica_groups=[[0, 1, 2, 3], [4, 5, 6, 7]],
                                         ins=[src_t[k].ap().opt()], outs=[gat_t[k].ap().opt()]).then_inc(csem)
            issued.append(k)

        def cc_hook(c):
            done_c = c - 2
            if done_c == 3 * NPS - 1 and 0 not in issued:
                issue_cc(0)
            if done_c >= 3 * NPS:
                k = 1 + (done_c - 3 * NPS)
                if k not in issued:
                    issue_cc(k)

        _ow = out_writer

        def out_writer2(cx_, obf_t, b_obf_, key):
            if b_obf_ not in obf_bufs:
                obf_bufs.append(b_obf_)
            _ow(cx_, obf_t, b_obf_, key)

        emit_AB(nc, cx, S, pbank, b_bank, pbT, b_bankT, out_writer2, cc_hook)
        def ensure_cc(upto):
            for k in range(min(upto, NPC - 1) + 1):
                if k not in issued:
                    issue_cc(k)
        ensure_cc(1)
        gat_rows = [t.ap().rearrange("f (s w) -> (f s) w", s=4) for t in gat_t]
        cc_waited = [0]

        def oT_loader(cx_, oTg, b_oTg, tok0, N):
            k = 0 if tok0 == 0 else 1 + (tok0 - 128) // 512
            ensure_cc(k + 2)
            need = issued.index(k) + 1
            if cc_waited[0] < need:
                nc.gpsimd.wait_ge(csem, need)
                cc_waited[0] = need
            for cg in range(8):
                cx_.dma(pool, oTg[:, cg, :N], gat_rows[k], dst=b_oTg, reads=[b_idx],
                        indirect=bass.IndirectOffsetOnAxis(idxsb[:, cg:cg + 1], 0))

        emit_C(nc, cx, groups, 1, pbank, b_bank, pbT, b_bankT, oT_loader, D)
    return nc


_PROG = {}


def kernel(**inp):
    inp = {k_: np.asarray(v) for k_, v in inp.items()}
    Bn, S, D = inp["x"].shape
    TPC = S // 4
    consts = ab_constants(S)
    if S not in _PROG:
        _PROG[S] = build_fused(S)
    wi = c_weight_inputs(inp)
    perm = [(cg // 2) if cg % 2 == 0 else 4 + cg // 2 for cg in range(8)]
    wi["wout"] = np.ascontiguousarray(wi["wout"][:, perm, :])
    in_maps = []
    for c in range(8):
        b, j = c // 4, c % 4
        m = ab_core_inputs(inp, b, j, S, consts)
        m.update(wi)
        t0 = TPC * j - 128
        ntok = TPC + 128
        xt = np.zeros((ntok, 1024), np.float32)
        lo = max(t0, 0)
        xt[lo - t0:] = inp["x"][b, lo:t0 + ntok]
        m["xt"] = xt
        m["invtab"] = inv_table(j == 0)
        p = np.arange(128)[:, None]
        cg = np.arange(8)[None, :]
        m["oidx"] = ((cg * 128 + p) * 4 + j).astype(np.uint32)
        in_maps.append(m)
    res = run_bass_kernel_spmd(_PROG[S], in_maps, core_ids=list(range(8)))
    out = np.zeros((Bn, S, D), np.float32)
    for c in range(8):
        b, j = c // 4, c % 4
        out[b, TPC * j:TPC * (j + 1)] = res.results[c]["out"]
    return out
```

```python
import contextlib
import numpy as np
import ml_dtypes
import concourse.bass as bass
import concourse.mybir as mybir
from concourse.bass_utils import run_bass_kernel_spmd

F32 = mybir.dt.float32
BF16 = mybir.dt.bfloat16
AF = mybir.ActivationFunctionType
ALU = mybir.AluOpType
NEG = -30000.0


class Tok:
    __slots__ = ("sem", "val", "pe")

    def __init__(self, sem, val, pe=False):
        self.sem = sem
        self.val = val
        self.pe = pe


class Buf:
    def __init__(self, ctx, name, dma=False):
        self.name = name
        self.w = None
        self.r = {}
        self.sem = None
        self.cnt = 0
        if dma:
            self.sem = ctx.new_sem("b_" + name)


class Eng:
    def __init__(self, ctx, eng, name, is_pe=False):
        self.ctx = ctx
        self.eng = eng
        self.name = name
        self.is_pe = is_pe
        self.sem = ctx.new_sem("e_" + name)
        self.count = 0
        self.waited = {}

    def wait(self, deps):
        for t in deps:
            if t is None:
                continue
            if self.is_pe and t.sem is self.sem:
                continue
            key = id(t.sem)
            if self.waited.get(key, 0) >= t.val:
                continue
            self.eng.wait_ge(t.sem, t.val)
            self.waited[key] = t.val


class Ctx:
    def __init__(self, nc):
        self.nc = nc
        self.es = contextlib.ExitStack()
        self.nsem = 0
        self.act = Eng(self, nc.scalar, "act")
        self.dve = Eng(self, nc.vector, "dve")
        self.pool = Eng(self, nc.gpsimd, "pool")
        self.pe = Eng(self, nc.tensor, "pe", is_pe=True)
        self.sp = Eng(self, nc.sync, "sp")
        self.n_inst = 0
        self.sbes = self.es
        self.allbufs = []

    def global_barrier(self):
        toks = [Tok(e.sem, e.count) for e in (self.act, self.dve, self.pool, self.pe) if e.count > 0]
        toks += [Tok(b.sem, b.cnt) for b in self.allbufs if b.sem is not None and b.cnt > 0]
        for e in (self.sp, self.act, self.dve, self.pool, self.pe):
            e.wait(toks)

    def new_sem(self, name):
        self.nsem += 1
        return self.es.enter_context(self.nc.semaphore(name))

    def sb(self, name, shape, dt):
        self.nsb = getattr(self, "nsb", 0) + 1
        return self.sbes.enter_context(self.nc.sbuf_tensor(f"sb{self.nsb}_" + name, shape, dt))

    def ps(self, name, shape, dt):
        return self.es.enter_context(self.nc.psum_tensor("ps_" + name, shape, dt))

    def buf(self, name, dma=False):
        b = Buf(self, name, dma)
        self.allbufs.append(b)
        return b

    def _deps(self, reads, writes):
        deps = []
        for b in reads:
            if b.w is not None:
                deps.append(b.w)
        for b in writes:
            if b.w is not None:
                deps.append(b.w)
            deps.extend(b.r.values())
        return deps

    def _commit(self, tok, reads, writes):
        for b in reads:
            b.r[id(tok.sem)] = tok
        for b in writes:
            b.w = tok
            b.r = {}

    def op(self, eng, fn, *a, reads=(), writes=(), **kw):
        eng.wait(self._deps(reads, writes))
        inst = getattr(eng.eng, fn)(*a, **kw)
        eng.count += 1
        inst.then_inc(eng.sem, 1)
        tok = Tok(eng.sem, eng.count)
        self._commit(tok, reads, writes)
        self.n_inst += 1
        return tok

    def dma(self, q, out, in_, dst=None, src=None, reads=(), writes=(), indirect=None, **kw):
        reads = list(reads) + ([src] if src is not None else [])
        writes = list(writes) + ([dst] if dst is not None else [])
        semb = dst if (dst is not None and dst.sem is not None) else src
        assert semb is not None and semb.sem is not None
        q.wait(self._deps(reads, writes))
        semb.cnt += 16
        if indirect is not None:
            q.eng.indirect_dma_start(out=out, out_offset=None, in_=in_, in_offset=indirect, **kw).then_inc(semb.sem, 16)
        else:
            q.eng.dma_start(out=out, in_=in_, **kw).then_inc(semb.sem, 16)
        tok = Tok(semb.sem, semb.cnt)
        self._commit(tok, reads, writes)
        self.n_inst += 1
        return tok


def bf16_np(a):
    return np.asarray(a).astype(ml_dtypes.bfloat16)

NFC = 22


def c_drams(nc, NTOK, NOUT):
    d = {}
    d["x"] = nc.dram_tensor("xt", [NTOK, 1024], F32, kind="ExternalInput").ap()
    d["wout"] = nc.dram_tensor("wout", [128, 8, 1024], F32, kind="ExternalInput").ap()
    d["wg"] = nc.dram_tensor("wg", [2, NFC, 128, 1024], F32, kind="ExternalInput").ap()
    d["wu"] = nc.dram_tensor("wu", [2, NFC, 128, 1024], F32, kind="ExternalInput").ap()
    d["wd"] = nc.dram_tensor("wd", [2, NFC, 128, 1024], F32, kind="ExternalInput").ap()
    d["pw"] = nc.dram_tensor("poolw", [128, 8, 256], F32, kind="ExternalInput").ap()
    d["gv"] = nc.dram_tensor("gvec", [8, 1024], F32, kind="ExternalInput").ap()
    d["it"] = nc.dram_tensor("invtab", [4, 16], F32, kind="ExternalInput").ap()
    d["out"] = nc.dram_tensor("out", [NOUT, 1024], F32, kind="ExternalOutput").ap()
    d["s_wg"] = nc.dram_tensor("wg_s", [2, NFC, 128, 1024], BF16, kind="Internal").ap()
    d["s_wu"] = nc.dram_tensor("wu_s", [2, NFC, 128, 1024], BF16, kind="Internal").ap()
    d["s_wd"] = nc.dram_tensor("wd_s", [2, NFC, 128, 1024], BF16, kind="Internal").ap()
    d["s_wout"] = nc.dram_tensor("wout_s", [128, 8, 1024], BF16, kind="Internal").ap()
    d["s_pw"] = nc.dram_tensor("poolw_s", [128, 8, 256], BF16, kind="Internal").ap()
    return d


def emit_prepass(nc, cx, D):
    pool = cx.pool
    NS = 2
    stage = [cx.sb(f"stage{i}", [128, 1024], F32) for i in range(NS)]
    stageb = [cx.sb(f"stageb{i}", [128, 1024], BF16) for i in range(NS)]
    b_stage = [cx.buf(f"stage{i}", True) for i in range(NS)]
    b_stageb = [cx.buf(f"stageb{i}", True) for i in range(NS)]
    b_scr = cx.buf("wscr")
    k = [0]

    def cast_slab(src_ap, dst_dram_ap):
        i = k[0] % NS
        cx.dma(pool, stage[i][:], src_ap, dst=b_stage[i])
        cx.op(pool, "tensor_copy", stageb[i][:], stage[i][:], reads=[b_stage[i]], writes=[b_stageb[i]])
        cx.dma(pool, dst_dram_ap, stageb[i][:], src=b_stageb[i], writes=[b_scr])
        k[0] += 1

    for c in range(8):
        cast_slab(D["wout"][:, c, :], D["s_wout"][:, c, :])
    for hh in range(2):
        cast_slab(D["pw"][:, 4 * hh:4 * hh + 4, :].rearrange("p a b -> p (a b)"),
                  D["s_pw"][:, 4 * hh:4 * hh + 4, :].rearrange("p a b -> p (a b)"))
    for l in range(2):
        for fc in range(NFC):
            cast_slab(D["wg"][l, fc], D["s_wg"][l, fc])
            cast_slab(D["wu"][l, fc], D["s_wu"][l, fc])
            cast_slab(D["wd"][l, fc], D["s_wd"][l, fc])


def emit_C(nc, cx, groups, n_skip_tiles, pbank, b_bank_, pbT, b_bankT_, oT_loader, D):
    NT = sum(groups)
    d_x, d_gv, d_it, d_out = D["x"], D["gv"], D["it"], D["out"]
    s_wg, s_wu, s_wd = D["s_wg"], D["s_wu"], D["s_wd"]
    c_es = contextlib.ExitStack()
    cx.sbes = c_es
    if True:
        act, dve, pool, pe, sp = cx.act, cx.dve, cx.pool, cx.pe, cx.sp
        NMAX = max(groups) * 128
        RING = 2
        wout = cx.sb("wout", [128, 8, 1024], BF16)
        poolw = cx.sb("poolw", [128, 8, 256], BF16)
        gv = cx.sb("gv", [128, 8, 1024], F32)
        invt = cx.sb("invt", [128, 4, 16], F32)
        ident = cx.sb("ident", [128, 128], BF16)
        identf = cx.sb("identf", [128, 128], F32)
        pm = [pbank[0], pbank[1]]
        pg = [pbank[2], pbank[3]]
        pu = [pbank[4], pbank[5]]
        pT = pbT[:].rearrange("p (c k) -> p c k", c=8)
        pTf = pbank[6][:].rearrange("p (c k) -> p c k", c=4)
        B = lambda n, dma=False: cx.buf(n, dma)
        b_wout, b_poolw, b_gv, b_invt, b_id = B("wout", True), B("poolw", True), B("gv", True), B("invt", True), B("id")
        b_oTg, b_hg = B("oTg", True), [B(f"hg{i}", True) for i in range(max(groups))]
        b_tmp, b_ub, b_uf, b_uTg, b_aTg = B("tmp"), B("ub"), B("uf"), B("uTg"), B("aTg")
        b_junk = b_ub
        b_sg = [B("sg0"), B("sg1")]
        b_wgr = [B(f"wgr{i}", True) for i in range(RING)]
        b_wdall = B("wdall", True)
        b_ss = B("ss")
        b_u1T, b_pa, b_pb, b_plT = B("u1T"), B("pa"), B("pb"), b_oTg
        b_pg, b_pu = [B("pg0"), B("pg1")], [B("pu0"), B("pu1")]
        b_pmA = B("pmA")
        pmsets = [(pm, b_pmA), (pg, None)]
        b_pT, b_pTf = B("pT"), B("pTf")
        b_scr = B("scr")

        cx.op(pool, "memset", identf[:], 0.0, writes=[b_id])
        cx.op(pool, "affine_select", identf[:], identf[:], [[-1, 128]], ALU.not_equal, 1.0,
              base=0, channel_multiplier=1, reads=[b_id], writes=[b_id])
        cx.op(pool, "tensor_copy", ident[:], identf[:], reads=[b_id], writes=[b_id])
        for r in range(8):
            cx.dma(sp, gv[:, r, :], d_gv[r].partition_broadcast(128), dst=b_gv)
        for r in range(4):
            cx.dma(sp, invt[:, r, :], d_it[r].partition_broadcast(128), dst=b_invt)
        cx.dma(sp, wout[:], D["s_wout"], dst=b_wout)
        cx.dma(sp, poolw[:], D["s_pw"], dst=b_poolw)
        hg = cx.sb("hg", [128, max(groups), 1024], F32)
        tmp = cx.sb("tmp", [128, 1024], F32)
        ub = cx.sb("ub", [128, 1024], BF16)
        junk = ub
        uf = cx.sb("uf", [128, 1024], F32)
        uTg = cx.sb("uTg", [128, 8, NMAX], BF16)
        aTg = cx.sb("aTg", [128, NFC, NMAX], BF16)
        sg = [cx.sb(f"sg{i}", [128, NMAX], F32) for i in range(2)]
        wgr = [cx.sb(f"wgr{i}", [128, 1024], BF16) for i in range(RING)]
        wur = [cx.sb(f"wur{i}", [128, 1024], BF16) for i in range(RING)]
        wdall = cx.sb("wdall", [128, NFC, 1024], BF16)
        ss = cx.sb("ss", [128, 8], F32)
        u1T = cx.sb("u1T", [128, 8, 16 + NMAX], F32)
        pa = cx.sb("pa", [128, 16 + NMAX], F32)
        pb = cx.sb("pb", [128, 16 + NMAX], F32)
        oTg = cx.sb("oTg", [128, 8, NMAX], BF16)
        plT = oTg
        cx.op(pool, "memset", u1T[:], 0.0, writes=[b_u1T])
        GI = {"g01": 0, "g02": 1, "g03": 2, "g10": 3, "g11": 4, "g12": 5, "g13": 6, "psc": 7}
        nss = [0]

        def rms_scale(src_ap, src_buf, gname, out_ap, out_buf, extra_reads=()):
            j = nss[0] % 8
            nss[0] += 1
            cx.op(act, "activation", junk[:], src_ap, AF.Square, accum_out=ss[:, j:j + 1],
                  reads=[src_buf], writes=[b_junk, b_ss])
            cx.op(act, "activation", ss[:, j:j + 1], ss[:, j:j + 1], AF.Sqrt, bias=1e-6, scale=1.0 / 1024,
                  reads=[b_ss], writes=[b_ss])
            cx.op(dve, "reciprocal", ss[:, j:j + 1], ss[:, j:j + 1], reads=[b_ss], writes=[b_ss])
            cx.op(dve, "scalar_tensor_tensor", out_ap, src_ap, ss[:, j:j + 1], gv[:, GI[gname], :], ALU.mult, ALU.mult,
                  reads=[src_buf, b_ss, b_gv] + list(extra_reads), writes=[out_buf])

        def transpose_bf(src_sb, src_buf, dstT, dst_buf, col0):
            for c in range(8):
                cx.op(pe, "transpose", pT[:, c, :], src_sb[:, c * 128:(c + 1) * 128], ident[:],
                      reads=[src_buf, b_id], writes=[b_pT])
            cx.op(act, "copy", dstT[:, :, col0:col0 + 128], pT, reads=[b_pT], writes=[dst_buf])

        def ffn(layer, T, gname_post):
            N = T * 128
            cx.dma(sp, wdall[:], s_wd[layer].rearrange("f p n -> p f n"), dst=b_wdall, reads=[b_scr])
            for fc in range(NFC):
                r = fc % RING
                cx.dma(sp, wgr[r][:], s_wg[layer, fc], dst=b_wgr[r], reads=[b_scr])
                cx.dma(sp, wur[r][:], s_wu[layer, fc], dst=b_wgr[r], reads=[b_scr])
                pb_ = fc % 2
                for c in range(8):
                    cx.op(pe, "matmul", pg[pb_][:, :N], wgr[r][:, c * 128:(c + 1) * 128], uTg[:, c, :N],
                          start=(c == 0), stop=(c == 7), reads=[b_wgr[r], b_uTg], writes=[b_pg[pb_]])
                for c in range(8):
                    cx.op(pe, "matmul", pu[pb_][:, :N], wur[r][:, c * 128:(c + 1) * 128], uTg[:, c, :N],
                          start=(c == 0), stop=(c == 7), reads=[b_wgr[r], b_uTg], writes=[b_pu[pb_]])
                cx.op(act, "activation", sg[pb_][:, :N], pg[pb_][:, :N], AF.Silu, reads=[b_pg[pb_]], writes=[b_sg[pb_]])
                cx.op(dve, "tensor_tensor", aTg[:, fc, :N], sg[pb_][:, :N], pu[pb_][:, :N], ALU.mult,
                      reads=[b_sg[pb_], b_pu[pb_]], writes=[b_aTg])
            def down(t):
                pmt, wr = pmset(t)
                for hf in range(2):
                    for fc in range(NFC):
                        cx.op(pe, "matmul", pmt[hf][:], aTg[:, fc, t * 128:(t + 1) * 128], wdall[:, fc, hf * 512:(hf + 1) * 512],
                              start=(fc == 0), stop=(fc == NFC - 1), reads=[b_aTg, b_wdall], writes=wr[hf])
            down(0)
            for t in range(T):
                if t + 1 < T:
                    down(t + 1)
                post_norm_add(t, gname_post)

        def pmset(t):
            if t % 2 == 0:
                return pm, [[b_pmA], [b_pmA]]
            return pg, [[b_pg[0]], [b_pg[1]]]

        def post_norm_add(t, gname):
            pm, wr_ = pmset(t)
            b_pm0, b_pm1 = wr_[0][0], wr_[1][0]
            j = nss[0] % 8
            nss[0] += 1
            cx.op(act, "activation", junk[:, 0:512], pm[0][:], AF.Square, accum_out=ss[:, j:j + 1],
                  reads=[b_pm0], writes=[b_junk, b_ss])
            j2 = nss[0] % 8
            nss[0] += 1
            cx.op(act, "activation", junk[:, 512:1024], pm[1][:], AF.Square, accum_out=ss[:, j2:j2 + 1],
                  reads=[b_pm1], writes=[b_junk, b_ss])
            cx.op(dve, "tensor_tensor", ss[:, j:j + 1], ss[:, j:j + 1], ss[:, j2:j2 + 1], ALU.add, reads=[b_ss], writes=[b_ss])
            cx.op(act, "activation", ss[:, j:j + 1], ss[:, j:j + 1], AF.Sqrt, bias=1e-6, scale=1.0 / 1024,
                  reads=[b_ss], writes=[b_ss])
            cx.op(dve, "reciprocal", ss[:, j:j + 1], ss[:, j:j + 1], reads=[b_ss], writes=[b_ss])
            for hf in range(2):
                cx.op(dve, "scalar_tensor_tensor", tmp[:, hf * 512:(hf + 1) * 512], pm[hf][:], ss[:, j:j + 1],
                      gv[:, GI[gname], hf * 512:(hf + 1) * 512], ALU.mult, ALU.mult,
                      reads=[wr_[hf][0], b_ss, b_gv], writes=[b_tmp])
            cx.op(pool, "tensor_tensor", hg[:, t, :], hg[:, t, :], tmp[:], ALU.add, reads=[b_tmp], writes=[b_hg[t]])

        tile0 = 0
        for gi_, T in enumerate(groups):
            N = T * 128
            tok0 = tile0 * 128
            oT_loader(cx, oTg, b_oTg, tok0, N)
            for t in range(T):
                cx.dma(sp, hg[:, t, :], d_x[tok0 + t * 128: tok0 + (t + 1) * 128, :], dst=b_hg[t])
            def mproj(t):
                pmt, wr = pmset(t)
                for hf in range(2):
                    for c in range(8):
                        cx.op(pe, "matmul", pmt[hf][:], oTg[:, c, t * 128:(t + 1) * 128], wout[:, c, hf * 512:(hf + 1) * 512],
                              start=(c == 0), stop=(c == 7), reads=[b_oTg, b_wout], writes=wr[hf])
            mproj(0)
            for t in range(T):
                post_norm_add(t, "g01")
                rms_scale(hg[:, t, :], b_hg[t], "g02", ub[:], b_ub)
                if t + 1 < T:
                    mproj(t + 1)
                transpose_bf(ub, b_ub, uTg, b_uTg, t * 128)
            ffn(0, T, "g03")
            for t in range(T):
                rms_scale(hg[:, t, :], b_hg[t], "g10", uf[:], b_uf)
                for q4 in range(2):
                    for c in range(4):
                        cc = q4 * 4 + c
                        cx.op(pe, "transpose", pTf[:, c, :], uf[:, cc * 128:(cc + 1) * 128], identf[:],
                              reads=[b_uf, b_id], writes=[b_pTf])
                    cx.op(act, "copy", u1T[:, q4 * 4:q4 * 4 + 4, 16 + t * 128:16 + (t + 1) * 128], pTf,
                          reads=[b_pTf], writes=[b_u1T])
            halo_only = bool(n_skip_tiles) and gi_ == 0
            if not halo_only:
                W = 16 + N
                for c in range(8):
                    lw = c // 2 + 1
                    w = 1 << lw
                    src = u1T[:, c, :W]
                    srcb = b_u1T
                    bufs = [(pa, b_pa), (pb, b_pb)]
                    for lv in range(lw):
                        sh = 1 << lv
                        dst, dstb = bufs[lv % 2]
                        cx.op(dve, "tensor_tensor", dst[:, sh:W], src[:, sh:W], src[:, 0:W - sh], ALU.add,
                              reads=[srcb], writes=[dstb])
                        src, srcb = dst[:, :W], dstb
                    cx.op(dve, "scalar_tensor_tensor", plT[:, c, :N], src[:, 16:W], 1.0 / w, u1T[:, c, 16:W],
                          ALU.mult, ALU.subtract, reads=[srcb, b_u1T], writes=[b_plT])
                    if gi_ == (1 if n_skip_tiles else 0):
                        cx.op(dve, "tensor_tensor", tmp[:, 0:16], src[:, 16:32], invt[:, c // 2, :], ALU.mult,
                              reads=[srcb, b_invt], writes=[b_tmp])
                        cx.op(dve, "tensor_tensor", plT[:, c, 0:16], tmp[:, 0:16], u1T[:, c, 16:32], ALU.subtract,
                              reads=[b_tmp, b_u1T], writes=[b_plT])
            cx.op(pool, "tensor_copy", pa[:, 0:128].rearrange("p (c k) -> p c k", c=8), u1T[:, :, N:N + 16],
                  reads=[b_u1T], writes=[b_pa])
            cx.op(pool, "tensor_copy", u1T[:, :, 0:16], pa[:, 0:128].rearrange("p (c k) -> p c k", c=8),
                  reads=[b_pa], writes=[b_u1T])
            if not halo_only:
                for t in range(T):
                    for g4 in range(4):
                        for cc in range(2):
                            hf = g4 // 2
                            cx.op(pe, "matmul", pm[hf][:, (g4 % 2) * 256:(g4 % 2) * 256 + 256],
                                  plT[:, 2 * g4 + cc, t * 128:(t + 1) * 128], poolw[:, 2 * g4 + cc, :],
                                  start=(cc == 0), stop=(cc == 1), reads=[b_plT, b_poolw], writes=[b_pmA])
                    for hf in range(2):
                        cx.op(dve, "tensor_tensor", uf[:, hf * 512:(hf + 1) * 512], pm[hf][:], gv[:, GI["psc"], hf * 512:(hf + 1) * 512],
                              ALU.mult, reads=[b_pmA, b_gv], writes=[b_uf])
                    rms_scale(uf[:], b_uf, "g11", tmp[:], b_tmp)
                    cx.op(pool, "tensor_tensor", hg[:, t, :], hg[:, t, :], tmp[:], ALU.add, reads=[b_tmp], writes=[b_hg[t]])
                    rms_scale(hg[:, t, :], b_hg[t], "g12", ub[:], b_ub)
                    transpose_bf(ub, b_ub, uTg, b_uTg, t * 128)
                ffn(1, T, "g13")
            for t in range(T):
                gt = tile0 + t
                if gt >= n_skip_tiles:
                    o0 = (gt - n_skip_tiles) * 128
                    cx.dma(sp, d_out[o0:o0 + 128, :], hg[:, t, :], src=b_hg[t])
            tile0 += T
        cx.global_barrier()
    c_es.close()
    cx.sbes = cx.es


def c_weight_inputs(inp):
    d = {}
    d["wout"] = np.ascontiguousarray(inp["attn_w_out"][0].reshape(8, 128, 1024).transpose(1, 0, 2))
    wg = inp["ffn_w_gate"].reshape(2, 8, 128, NFC, 128)
    d["wg"] = np.ascontiguousarray(wg.transpose(0, 3, 2, 1, 4).reshape(2, NFC, 128, 1024))
    wu = inp["ffn_w_up"].reshape(2, 8, 128, NFC, 128)
    d["wu"] = np.ascontiguousarray(wu.transpose(0, 3, 2, 1, 4).reshape(2, NFC, 128, 1024))
    d["wd"] = np.ascontiguousarray(inp["ffn_w_down"].reshape(2, NFC, 128, 1024))
    pw = inp["pool_w"][0].reshape(4, 2, 128, 256)
    d["poolw"] = np.ascontiguousarray(pw.transpose(2, 0, 1, 3).reshape(128, 8, 256))
    ng = inp["norm_g"]
    d["gvec"] = np.ascontiguousarray(np.stack([ng[0, 1], ng[0, 2], ng[0, 3], ng[1, 0], ng[1, 1], ng[1, 2], ng[1, 3],
                                               inp["pool_scale"][0]]).astype(np.float32))
    return d


def inv_table(seq_start):
    t = np.zeros((4, 16), np.float32)
    for gi, w in enumerate((2, 4, 8, 16)):
        for i in range(16):
            t[gi, i] = 1.0 / (min(i + 1, w) if seq_start else w)
    return t

BIGV = 1e30
NEGINF = -1e30
NFM = 776
SLOPES = [2.0 ** (-(i + 1)) for i in range(8)]


def emit_AB(nc, cx, S, pbank, b_bank, pbT, b_bankT, out_writer, cc_hook=None):
    NTL = S // 128
    NCH = S // 512
    NC = S // 16 - 1
    NCT = (NC + 127) // 128
    NCP = NCT * 128
    NB = S // 64
    NJT = (NB + 127) // 128
    DI = lambda n, sh, dt: nc.dram_tensor(n, sh, dt, kind="ExternalInput").ap()
    DS = lambda n, sh, dt: nc.dram_tensor(n, sh, dt, kind="Internal").ap()
    d_x = DI("x", [S, 1024], F32)
    d_g0 = DI("g0", [1024], F32)
    d_wfm = DI("wfm", [128, 8, NFM], F32)
    d_wtm = DI("wtm", [128, 8, 256], F32)
    d_nbf = DI("bf", [2, 1], F32)
    d_pe = DI("cpe", [2, 64, 32], F32)
    d_w1 = DI("cw1", [2, 64, 32 * 64], F32)
    d_w2 = DI("cw2", [2, 64, 64], F32)
    d_alq = DI("alibiQ", [4, 4, S], BF16)
    d_posk = DI("posK", [4, S], BF16)
    d_blkk = DI("blkK", [2, S], BF16)
    d_cmpk = DI("cmpK", [4, NCP], BF16)
    d_tri = DI("tri", [128, 2, 128], BF16)
    d_id = DI("identb", [128, 128], BF16)
    d_idf = DI("identf", [128, 128], F32)
    d_lst = DI("lstrict", [128, 128], F32)
    d_cmask = DI("cmpmask", [128, 5, 512], BF16)
    d_spatch = DI("selpatch", [128, 10], BF16)
    d_k3 = DI("keepadd3", [128, 2, 3], F32)
    s_qf = DS("s_qf", [2, 64, S], BF16)
    s_kf = DS("s_kf", [2, 64, S], BF16)
    s_qn = DS("s_qn", [4, 64, S], BF16)
    s_kv = DS("s_kv", [4, 64, S], BF16)
    s_vv = DS("s_vv", [S, 256], BF16)
    s_lf = DS("s_lf", [2, S], F32)
    s_gt = DS("s_gt", [6, S], F32)
    s_cr = DS("s_cr", [2, 6, S], BF16)
    s_selb = DS("s_selb", [NJT, 128, S], BF16)

    ab_es = contextlib.ExitStack()
    cx.sbes = ab_es
    if True:
        act, dve, pool, pe, sp = cx.act, cx.dve, cx.pool, cx.pe, cx.sp
        B = lambda n, dma=False: cx.buf(n, dma)
        ident = cx.sb("ident", [128, 128], BF16)
        identf = cx.sb("identf", [128, 128], F32)
        tri = cx.sb("tri", [128, 2, 128], BF16)
        ones65 = cx.sb("ones65", [128, 64], F32)
        b_const = B("const", True)
        cx.dma(sp, ident[:], d_id, dst=b_const)
        cx.dma(sp, identf[:], d_idf, dst=b_const)
        cx.dma(sp, tri[:], d_tri, dst=b_const)
        b_ones = B("ones65")
        cx.op(dve, "memset", ones65[:], 1.0, writes=[b_ones])
        b_scr = B("scr")

        st1 = contextlib.ExitStack()
        S1 = lambda n, sh, dt: st1.enter_context(nc.sbuf_tensor("s1_" + n, sh, dt))
        wfm = S1("wfm", [128, 8, NFM], BF16)
        wtm = S1("wtm", [128, 8, 256], BF16)
        g0 = S1("g0", [128, 1024], F32)
        nbf = S1("nbf", [8, 1], F32)
        stg = [S1(f"stg{i}", [128, 1024], F32) for i in range(2)]
        xs = [S1(f"xs{i}", [128, 1024], F32) for i in range(8)]
        junk = S1("junk", [128, 1024], BF16)
        ub = [S1(f"ub{i}", [128, 1024], BF16) for i in range(3)]
        ss = S1("ss", [128, 8], F32)
        uTg = [S1(f"uTg{i}", [128, 8, 512], BF16) for i in range(2)]
        ev = [S1(f"ev{i}", [128, 512], BF16) for i in range(4)]
        vsb = [S1(f"vsb{i}", [128, 256], BF16) for i in range(2)]
        sm = [S1(f"sm{i}", [8, 512], F32) for i in range(4)]
        b_wfm, b_wtm, b_g0, b_nbf = B("wfm"), B("wtm"), B("g0", True), B("nbf", True)
        b_stg = [B(f"stg{i}", True) for i in range(2)]
        b_xs = [B(f"xs{i}", True) for i in range(8)]
        b_junk, b_ss = B("junk"), B("ss")
        b_ub = [B("ub0"), B("ub1"), B("ub2")]
        b_uTg = [B("uTg0"), B("uTg1")]
        b_ev = [B(f"ev{i}", True) for i in range(4)]
        b_vsb = [B(f"vsb{i}", True) for i in range(2)]
        b_sm = [B(f"sm{i}", True) for i in range(4)]

        cx.dma(sp, g0[:], d_g0.partition_broadcast(128), dst=b_g0)
        cx.op(dve, "memset", nbf[:], 0.0, writes=[b_nbf])
        cx.dma(sp, nbf[0:2, :], d_nbf, dst=b_nbf)
        cx.op(dve, "tensor_scalar", nbf[:], nbf[:], -1.0, None, ALU.mult, reads=[b_nbf], writes=[b_nbf])
        k = 0
        for c in range(8):
            i = k % 2
            cx.dma(sp, stg[i][:, :NFM], d_wfm[:, c, :], dst=b_stg[i])
            cx.op(dve, "tensor_copy", wfm[:, c, :], stg[i][:, :NFM], reads=[b_stg[i]], writes=[b_wfm])
            k += 1
        for c in range(8):
            i = k % 2
            cx.dma(sp, stg[i][:, :256], d_wtm[:, c, :], dst=b_stg[i])
            cx.op(dve, "tensor_copy", wtm[:, c, :], stg[i][:, :256], reads=[b_stg[i]], writes=[b_wtm])
            k += 1

        pTv = pbT[:]
        fm_groups = [(0, 128, 0.125, s_qf, (0, 1)), (128, 128, 1.0, s_kf, (0, 1)), (256, 128, 0.125, s_qn, (0, 1)),
                     (384, 128, 0.125, s_qn, (2, 3)), (512, 128, 1.0, s_kv, (0, 1)), (640, 128, 1.0, s_kv, (2, 3))]
        nss = 0
        evi = 0
        smi = 0
        pfi = 0
        NTT = NCH * 4
        pTv2 = [pbT[:], pbank[0][:].bitcast(BF16)]
        b_pTv2 = [b_bankT, b_bank[0]]
        pV2 = [pbank[1], pbank[6]]
        b_pV2 = [b_bank[1], b_bank[6]]
        ss_of = {}

        def stA(tile_i):
            nonlocal nss
            xi = tile_i % 8
            j = nss % 8
            nss += 1
            cx.op(act, "activation", junk[:], xs[xi][:], AF.Square, accum_out=ss[:, j:j + 1],
                  reads=[b_xs[xi]], writes=[b_junk, b_ss])
            cx.op(act, "activation", ss[:, j:j + 1], ss[:, j:j + 1], AF.Sqrt, bias=1e-6, scale=1.0 / 1024,
                  reads=[b_ss], writes=[b_ss])
            cx.op(dve, "reciprocal", ss[:, j:j + 1], ss[:, j:j + 1], reads=[b_ss], writes=[b_ss])
            u_, b_u = ub[tile_i % 3], b_ub[tile_i % 3]
            cx.op(dve, "scalar_tensor_tensor", u_[:], xs[xi][:], ss[:, j:j + 1], g0[:], ALU.mult, ALU.mult,
                  reads=[b_xs[xi], b_ss, b_g0], writes=[b_u])

        def stB(tile_i):
            gch, t = tile_i // 4, tile_i % 4
            ug, b_ug = uTg[gch % 2], b_uTg[gch % 2]
            u_, b_u = ub[tile_i % 3], b_ub[tile_i % 3]
            pv_, b_pv = pTv2[tile_i % 2], b_pTv2[tile_i % 2]
            for c in range(8):
                cx.op(pe, "transpose", pv_[:, c * 128:(c + 1) * 128], u_[:, c * 128:(c + 1) * 128], ident[:],
                      reads=[b_u, b_const], writes=[b_pv])
            cx.op(act, "copy", ug[:, :, t * 128:(t + 1) * 128], pv_.rearrange("p (c k) -> p c k", c=8),
                  reads=[b_pv], writes=[b_ug])
            pvv, b_pvv = pV2[tile_i % 2], b_pV2[tile_i % 2]
            for c in range(8):
                cx.op(pe, "matmul", pvv[:, 0:256], ug[:, c, t * 128:(t + 1) * 128], wtm[:, c, :],
                      start=(c == 0), stop=(c == 7), reads=[b_ug, b_wtm], writes=[b_pvv])
            vi = tile_i % 2
            cx.op(dve, "tensor_copy", vsb[vi][:], pvv[:, 0:256], reads=[b_pvv], writes=[b_vsb[vi]])
            cx.dma(sp, s_vv[tile_i * 128:(tile_i + 1) * 128, :], vsb[vi][:], src=b_vsb[vi], writes=[b_scr])

        def stC(gch):
            nonlocal evi, smi, pfi
            ug, b_ug = uTg[gch % 2], b_uTg[gch % 2]
            cols = slice(gch * 512, (gch + 1) * 512)
            for (c0, M, scl, dst, idx) in fm_groups:
                bk = 2 + pfi % 4
                pfi += 1
                for c in range(8):
                    cx.op(pe, "matmul", pbank[bk][0:M, :], wfm[:, c, c0:c0 + M], ug[:, c, :],
                          start=(c == 0), stop=(c == 7), reads=[b_wfm, b_ug], writes=[b_bank[bk]])
                e_ = evi % 4
                evi += 1
                if evi % 2:
                    cx.op(act, "activation", ev[e_][:], pbank[bk][:], AF.Copy, scale=scl, reads=[b_bank[bk]], writes=[b_ev[e_]])
                else:
                    cx.op(dve, "tensor_scalar", ev[e_][:], pbank[bk][:], scl, None, ALU.mult, reads=[b_bank[bk]], writes=[b_ev[e_]])
                cx.dma(sp, dst[idx[0], :, cols], ev[e_][0:64, :], src=b_ev[e_], writes=[b_scr])
                cx.dma(sp, dst[idx[1], :, cols], ev[e_][64:128, :], src=b_ev[e_], writes=[b_scr])
            bk = 2 + pfi % 4
            pfi += 1
            for c in range(8):
                cx.op(pe, "matmul", pbank[bk][0:8, :], wfm[:, c, 768:776], ug[:, c, :],
                      start=(c == 0), stop=(c == 7), reads=[b_wfm, b_ug], writes=[b_bank[bk]])
            s_ = smi % 4
            smi += 1
            s2_ = smi % 4
            smi += 1
            cx.op(act, "activation", sm[s_][0:8, :], pbank[bk][0:8, :], AF.Exp, bias=nbf[:], scale=-1.0,
                  reads=[b_bank[bk], b_nbf], writes=[b_sm[s_]])
            cx.op(act, "activation", sm[s2_][0:8, :], pbank[bk][0:8, :], AF.Sigmoid, reads=[b_bank[bk]], writes=[b_sm[s2_]])
            cx.op(act, "activation", sm[s_][0:8, :], sm[s_][0:8, :], AF.Ln, bias=1.0, scale=1.0,
                  reads=[b_sm[s_]], writes=[b_sm[s_]])
            cx.op(dve, "tensor_scalar", sm[s_][0:8, :], sm[s_][0:8, :], -1.0, None, ALU.mult, reads=[b_sm[s_]], writes=[b_sm[s_]])
            cx.dma(sp, s_lf[:, cols], sm[s_][0:2, :], src=b_sm[s_], writes=[b_scr])
            cx.dma(sp, s_gt[:, cols], sm[s2_][2:8, :], src=b_sm[s2_], writes=[b_scr])

        SKEW = 2
        PF = 5

        def stX(tile_i):
            xi = tile_i % 8
            cx.dma(sp, xs[xi][:], d_x[tile_i * 128:(tile_i + 1) * 128, :], dst=b_xs[xi])

        for i in range(min(PF, NTT)):
            stX(i)
        for i in range(NTT + SKEW):
            if i + PF < NTT:
                stX(i + PF)
            if i < NTT:
                stA(i)
            jt_ = i - SKEW
            if 0 <= jt_ < NTT:
                stB(jt_)
                if jt_ % 4 == 3:
                    stC(jt_ // 4)

        def full_barrier(bufs):
            fin = [t for b in bufs for t in list(b.r.values()) + ([b.w] if b.w is not None else [])]
            for e_ in (sp, act, dve, pool, pe):
                e_.wait(fin)

        full_barrier(b_ev + b_vsb + b_sm + b_bank + b_uTg + b_ub + [b_junk, b_ss, b_bankT])
        st1.close()

        KA = cx.sb("KA", [128, S], BF16)
        KBb = cx.sb("KB", [128, S], BF16)
        VA = cx.sb("VA", [128, NTL, 65], BF16)
        VB = cx.sb("VB", [128, NTL, 65], BF16)
        b_KA, b_KB, b_VA, b_VB = B("KA", True), B("KB", True), B("VA", True), B("VB", True)
        QC = [cx.sb(f"QC{i}", [128, 512], BF16) for i in range(3)]
        b_QC = [B(f"QC{i}", True) for i in range(3)]
        PT = [cx.sb(f"PT{i}", [128, 512], BF16) for i in range(4)]
        b_PT = [B(f"PT{i}") for i in range(4)]
        osb = [cx.sb(f"osb{i}", [128, 512], F32) for i in range(3)]
        b_osb = [B("osb0"), B("osb1"), B("osb2")]
        wrow = [cx.sb(f"wrow{i}", [128, 512], F32) for i in range(3)]
        b_wrow = [B("wrow0"), B("wrow1"), B("wrow2")]
        obf = [cx.sb(f"obf{i}", [64, 512], BF16) for i in range(2)]
        b_obf = [B("obf0", True), B("obf1", True)]
        onsa = cx.sb("onsa", [64, 512], F32)
        otmp = cx.sb("otmp", [64, 512], F32)
        b_onsa, b_otmp = B("onsa"), B("otmp")
        gsb = [cx.sb(f"gsb{i}", [128, 3, 512], F32) for i in range(2)]
        b_gsb = [B("gsb0", True), B("gsb1", True)]
        PS_S, PS_O, PS_B = [0, 1, 2], [3, 4], 5

        def load_K(dstT, dstB, src_ap, aug_fn):
            cx.dma(sp, dstT[0:64, :], src_ap, dst=dstB)
            aug_fn(dstT, dstB)

        def load_V(dstV, dstB, col0):
            cx.op(pool, "memset", dstV[:, :, 64:65], 1.0, writes=[dstB])
            for h0 in range(0, NTL, 16):
                h1 = min(NTL, h0 + 16)
                cx.dma(sp, dstV[:, h0:h1, 0:64],
                       s_vv[h0 * 128:h1 * 128, col0:col0 + 64].rearrange("(t k) d -> k t d", k=128), dst=dstB, reads=[b_scr])

        class Pipe:
            def __init__(self, sbanks=None):
                self.sb = sbanks if sbanks is not None else PS_S
                self.ns = len(self.sb)
                self.items = []
                self.n = 0
                self.deferred = []

            def add(self, it):
                i = self.n
                self.n += 1
                it["i"] = i
                sb_, pt_ = self.sb[i % self.ns], i % self.ns
                q0, q1 = it["q0"], it["q1"]
                ex = it.get("extras", [])
                K = it["K"]
                cx.op(pe, "matmul", pbank[sb_][:, q0:q1], it["kT"], it["qT"][0:K, q0:q1], start=True, stop=(len(ex) == 0),
                      reads=it["reads"], writes=[b_bank[sb_]])
                for n_, (l_, r_, a_, b_, rd_) in enumerate(ex):
                    cx.op(pe, "matmul", pbank[sb_][:, a_:b_], l_, r_, start=False, stop=(n_ == len(ex) - 1),
                          reads=rd_, writes=[b_bank[sb_]])
                cx.op(act, "activation", PT[pt_][:, q0:q1], pbank[sb_][:, q0:q1], AF.Exp,
                      reads=[b_bank[sb_]], writes=[b_PT[pt_]])
                self.items.append(it)
                if len(self.items) > self.ns - 1:
                    self._pv(self.items.pop(0))

            def _pv(self, it):
                i = it["i"]
                pt_ = i % self.ns
                q0, q1 = it["q0"], it["q1"]
                ob = it["obank"]
                cx.op(pe, "matmul", pbank[ob][0:65, q0:q1], it["v"], PT[pt_][:, q0:q1], start=it["first"], stop=it["last"],
                      reads=[b_PT[pt_]] + it["vreads"], writes=[b_bank[ob]])
                nd = []
                for (cnt, fn) in self.deferred:
                    if cnt <= 0:
                        fn()
                    else:
                        nd.append((cnt - 1, fn))
                self.deferred = nd
                if it["last"] and it.get("epi") is not None:
                    fnb = it["epi"]()
                    if fnb is not None:
                        self.deferred.append((6, fnb))

            def flush(self):
                while self.items:
                    self._pv(self.items.pop(0))
                for (_, fn) in self.deferred:
                    fn()
                self.deferred = []

        epi_n = [0]

        def make_epilogue(ob, gate_ap, gate_buf, mode, out_ap):
            def epi_a():
                e = epi_n[0] % 3
                epi_n[0] += 1
                cx.op(act, "copy", osb[e][0:65, :], pbank[ob][0:65, :], reads=[b_bank[ob]], writes=[b_osb[e]])
                cx.op(dve, "tensor_scalar", wrow[e][64:65, :], osb[e][64:65, :], 1e-30, None, ALU.max,
                      reads=[b_osb[e]], writes=[b_wrow[e]])
                cx.op(dve, "reciprocal", wrow[e][64:65, :], wrow[e][64:65, :], reads=[b_wrow[e]], writes=[b_wrow[e]])
                if gate_ap is not None:
                    cx.op(dve, "tensor_tensor", wrow[e][64:65, :], wrow[e][64:65, :], gate_ap, ALU.mult,
                          reads=[b_wrow[e], gate_buf], writes=[b_wrow[e]])

                def epi_b():
                    cx.op(pe, "matmul", pbank[PS_B][0:64, :], ones65[64:65, 0:64], wrow[e][64:65, :], start=True, stop=True,
                          reads=[b_wrow[e], b_ones], writes=[b_bank[PS_B]])
                    if mode == "fox":
                        o_ = (epi_n[0] + e) % 2
                        cx.op(dve, "tensor_tensor", obf[o_][:], osb[e][0:64, :], pbank[PS_B][0:64, :], ALU.mult,
                              reads=[b_osb[e], b_bank[PS_B]], writes=[b_obf[o_]])
                        out_writer(cx, obf[o_], b_obf[o_], out_ap)
                    elif mode == "first":
                        cx.op(dve, "tensor_tensor", onsa[:], osb[e][0:64, :], pbank[PS_B][0:64, :], ALU.mult,
                              reads=[b_osb[e], b_bank[PS_B]], writes=[b_onsa])
                    else:
                        cx.op(dve, "tensor_tensor", otmp[:], osb[e][0:64, :], pbank[PS_B][0:64, :], ALU.mult,
                              reads=[b_osb[e], b_bank[PS_B]], writes=[b_otmp])
                        if mode == "mid":
                            cx.op(pool, "tensor_tensor", onsa[:], onsa[:], otmp[:], ALU.add, reads=[b_otmp], writes=[b_onsa])
                        else:
                            o_ = e % 2
                            cx.op(dve, "tensor_tensor", obf[o_][:], onsa[:], otmp[:], ALU.add,
                                  reads=[b_otmp, b_onsa], writes=[b_obf[o_]])
                            out_writer(cx, obf[o_], b_obf[o_], out_ap)
                return epi_b
            return epi_a

        qci = [0]

        def load_qc(src_ap, aug_ap, naug, cols):
            i = qci[0] % 3
            qci[0] += 1
            cx.dma(sp, QC[i][0:64, :], src_ap[:, cols], dst=b_QC[i], reads=[b_scr])
            cx.dma(sp, QC[i][64:64 + naug, :], aug_ap[:, cols], dst=b_QC[i], reads=[b_scr])
            return QC[i], b_QC[i]

        with contextlib.ExitStack() as st2:
            S2 = lambda n, sh, dt: st2.enter_context(nc.sbuf_tensor("s2_" + n, sh, dt))
            lst = S2("lst", [128, 128], F32)
            lf = S2("lf", [128, 128], F32)
            onesf = S2("onesf", [128, 128], F32)
            cw = S2("cw", [128, 128], F32)
            off = S2("off", [128, 1], F32)
            r1 = S2("r1", [128, 128], F32)
            crs = S2("crs", [128, 6, 128], BF16)
            b_c2 = B("c2", True)
            b_crs = B("crs", True)
            cx.dma(sp, lst[:], d_lst, dst=b_c2)
            cx.op(pool, "memset", onesf[:], 1.0, writes=[b_c2])
            for h in range(2):
                cx.dma(sp, lf[0:NTL, :], s_lf[h].rearrange("(t k) -> t k", k=128), dst=b_c2, reads=[b_scr])
                cx.op(dve, "tensor_tensor_scan", cw[0:NTL, :], onesf[0:NTL, :], lf[0:NTL, :], 0.0, ALU.mult, ALU.add,
                      reads=[b_c2], writes=[b_c2])
                cx.op(pe, "matmul", pbank[6][0:NTL, 0:1], lst[0:NTL, 0:NTL], cw[0:NTL, 127:128], start=True, stop=True,
                      reads=[b_c2], writes=[b_bank[6]])
                cx.op(act, "copy", off[0:NTL, :], pbank[6][0:NTL, 0:1], reads=[b_bank[6]], writes=[b_c2])
                cx.op(dve, "tensor_scalar", cw[0:NTL, :], cw[0:NTL, :], off[0:NTL, :], None, ALU.add, reads=[b_c2], writes=[b_c2])
                cx.op(dve, "tensor_copy", crs[0:NTL, 0, :], cw[0:NTL, :], reads=[b_c2], writes=[b_crs])
                cx.op(dve, "tensor_tensor", r1[0:NTL, :], cw[0:NTL, :], crs[0:NTL, 0, :], ALU.subtract, reads=[b_c2, b_crs], writes=[b_c2])
                cx.op(dve, "tensor_copy", crs[0:NTL, 1, :], r1[0:NTL, :], reads=[b_c2], writes=[b_crs])
                cx.op(dve, "tensor_tensor", r1[0:NTL, :], r1[0:NTL, :], crs[0:NTL, 1, :], ALU.subtract, reads=[b_c2, b_crs], writes=[b_c2])
                cx.op(dve, "tensor_copy", crs[0:NTL, 2, :], r1[0:NTL, :], reads=[b_c2], writes=[b_crs])
                cx.op(dve, "tensor_scalar", crs[0:NTL, 3:6, :], crs[0:NTL, 0:3, :], -1.0, None, ALU.mult, reads=[b_crs], writes=[b_crs])
                cx.dma(sp, s_cr[h].rearrange("r (t k) -> t r k", k=128), crs[0:NTL, :, :], src=b_crs, writes=[b_scr])
            full_barrier([b_c2, b_crs])

        KCA = cx.sb("KCA", [128, NCP], BF16)
        VCA = cx.sb("VCA", [128, NCT, 65], BF16)
        b_KCA, b_VCA = B("KCA", True), B("VCA")
        with contextlib.ExitStack() as st3:
            S3 = lambda n, sh, dt: st3.enter_context(nc.sbuf_tensor("s3_" + n, sh, dt))
            w1f = S3("w1f", [64, 2048], F32)
            w1b = S3("w1b", [64, 32, 64], BF16)
            pef = S3("pef", [64, 32], F32)
            peb = S3("peb", [64, 32], BF16)
            w2f = S3("w2f", [64, 64], F32)
            w2b = S3("w2b", [64, 64], BF16)
            b1 = S3("b1", [64, 1], F32)
            hx = S3("hx", [64, NCP], F32)
            hy = S3("hy", [64, NCP], F32)
            hb = S3("hb", [64, NCP], BF16)
            b_c3 = B("c3", True)
            cx.op(pool, "memset", KCA[0:64, :], 0.0, writes=[b_KCA])
            cx.op(pool, "memset", VCA[:], 0.0, writes=[b_VCA])
            cx.op(pool, "memset", VCA[:, :, 64:65], 1.0, writes=[b_VCA])
            cx.op(pool, "memset", hb[:], 0.0, writes=[b_c3])
            cx.dma(sp, KCA[64:68, :], d_cmpk, dst=b_KCA)
            for kvi in range(2):
                cx.dma(sp, w1f[:], d_w1[kvi], dst=b_c3)
                cx.dma(sp, pef[:], d_pe[kvi], dst=b_c3)
                cx.dma(sp, w2f[:], d_w2[kvi], dst=b_c3)
                kin, b_kin = (KA, b_KA) if kvi == 0 else (KBb, b_KB)
                cx.dma(sp, kin[0:64, :], s_kv[kvi], dst=b_kin, reads=[b_scr])
                cx.op(dve, "tensor_copy", w1b[:].rearrange("p l j -> p (l j)"), w1f[:], reads=[b_c3], writes=[b_c3])
                cx.op(dve, "tensor_copy", peb[:], pef[:], reads=[b_c3], writes=[b_c3])
                cx.op(dve, "tensor_copy", w2b[:], w2f[:], reads=[b_c3], writes=[b_c3])
                for l in range(32):
                    cx.op(pe, "matmul", pbank[6][0:64, 0:1], w1b[:, l, :], peb[:, l:l + 1], start=(l == 0), stop=(l == 31),
                          reads=[b_c3], writes=[b_bank[6]])
                cx.op(act, "copy", b1[:], pbank[6][0:64, 0:1], reads=[b_bank[6]], writes=[b_c3])
                kin3 = kin[0:64, :].rearrange("p (n s) -> p n s", s=16)
                for n0 in range(0, NC, 512):
                    n1 = min(NC, n0 + 512)
                    for l in range(32):
                        cx.op(pe, "matmul", pbank[5][0:64, 0:n1 - n0], w1b[:, l, :], kin3[:, n0 + l // 16:n1 + l // 16, l % 16],
                              start=(l == 0), stop=(l == 31), reads=[b_c3, b_kin], writes=[b_bank[5]])
                    cx.op(act, "activation", hx[:, n0:n1], pbank[5][0:64, 0:n1 - n0], AF.Identity, bias=b1[:], scale=1.0,
                          reads=[b_bank[5], b_c3], writes=[b_c3])
                cx.op(dve, "tensor_tensor", hy[:, :NC], hx[:, :NC], hx[:, :NC], ALU.mult, reads=[b_c3], writes=[b_c3])
                cx.op(dve, "tensor_scalar", hy[:, :NC], hy[:, :NC], 0.044715, 1.0, ALU.mult, ALU.add, reads=[b_c3], writes=[b_c3])
                cx.op(dve, "tensor_tensor", hy[:, :NC], hy[:, :NC], hx[:, :NC], ALU.mult, reads=[b_c3], writes=[b_c3])
                cx.op(act, "activation", hy[:, :NC], hy[:, :NC], AF.Sigmoid, scale=1.5957691216057308, reads=[b_c3], writes=[b_c3])
                cx.op(dve, "tensor_tensor", hb[:, :NC], hy[:, :NC], hx[:, :NC], ALU.mult, reads=[b_c3], writes=[b_c3])
                if kvi == 0:
                    for n0 in range(0, NC, 512):
                        n1 = min(NC, n0 + 512)
                        cx.op(pe, "matmul", pbank[5][0:64, 0:n1 - n0], w2b[:], hb[:, n0:n1], start=True, stop=True,
                              reads=[b_c3], writes=[b_bank[5]])
                        cx.op(act, "copy", KCA[0:64, n0:n1], pbank[5][0:64, 0:n1 - n0], reads=[b_bank[5]], writes=[b_KCA])
                else:
                    for nt in range(NCT):
                        cx.op(pe, "matmul", pbank[5][:, 0:64], hb[:, nt * 128:(nt + 1) * 128], w2b[:], start=True, stop=True,
                              reads=[b_c3], writes=[b_bank[5]])
                        cx.op(act, "copy", VCA[:, nt, 0:64], pbank[5][:, 0:64], reads=[b_bank[5]], writes=[b_VCA])
            if NC % 128:
                pass
            full_barrier([b_c3, b_KA, b_KB, b_KCA, b_VCA])

        st4 = contextlib.ExitStack()
        if True:
            S4 = lambda n, sh, dt: st4.enter_context(nc.sbuf_tensor("s4_" + n, sh, dt))
            q4 = [S4(f"q4_{i}", [128, 4, 512], BF16) for i in range(2)]
            b_q4 = [B("q4_0", True), B("q4_1", True)]
            esb = [[S4(f"esb{p}_{i}", [128, 1024], F32) for i in range(2)] for p in range(2)]
            b_esb = [[B(f"esb{p}_{i}") for i in range(2)] for p in range(2)]
            lsum = [S4(f"lsum{p}", [128, 8], F32) for p in range(2)]
            IMP = [S4(f"IMP{p}", [128, 1040], F32) for p in range(2)]
            PSL = [S4(f"PSL{p}", [128, 256], F32) for p in range(2)]
            PS2 = [S4(f"PS2{p}", [128, 256], F32) for p in range(2)]
            m8 = [S4(f"m8{p}", [128, 16], F32) for p in range(2)]
            sbq = [S4(f"sbq{p}", [128, 256], BF16) for p in range(2)]
            selT = [S4(f"selT{i}", [128, 2, 128], BF16) for i in range(2)]
            b_selT = [B("selT0", True), B("selT1", True)]
            spatch = S4("spatch", [128, 10], BF16)
            k3 = S4("k3", [128, 2, 3], F32)
            b_s4 = B("s4", True)
            b_dv = [B("dv0"), B("dv1")]
            b_ls = [B("ls0"), B("ls1")]
            cx.dma(sp, spatch[:], d_spatch, dst=b_s4)
            cx.dma(sp, k3[:], d_k3, dst=b_s4)
            for p in range(2):
                cx.op(dve, "memset", IMP[p][:], 0.0, writes=[b_dv[p]])
                cx.op(dve, "memset", PSL[p][:], NEGINF, writes=[b_dv[p]])
            NBW = min(NB, 256)
            SB_ = [pbank[6][:], pbT[:].bitcast(F32)]
            b_SB = [b_bank[6], b_bankT]
            pTs_all = [pbank[6][:].bitcast(BF16), pbT[:]]
            b_pTs_all = [b_bank[6], b_bankT]

            pending_tail = [None, None]

            def qb_chain(c, sub, qq, b_qq):
                qb = 4 * c + sub
                p = qb % 2
                bv, bl = b_dv[p], b_ls[p]
                if pending_tail[p] is not None:
                    pt_fn = pending_tail[p]
                    pending_tail[p] = None
                    yield from pt_fn()
                NV = min(8 * qb + 7, NC)
                pa_, pb_ = 8 * qb - 2, 8 * qb + 8
                nh = 1 if NV <= 512 else 2
                for r in range(4):
                    eb, b_eb = esb[p][r % 2], b_esb[p][r % 2]
                    for half in range(nh):
                        lo, hi = 512 * half, min(NV, 512 * (half + 1))
                        a_, b_ = max(lo, pa_, 0), min(hi, pb_)
                        has_patch = b_ > a_
                        yield
                        cx.op(pe, "matmul", SB_[p][:, 0:hi - lo], qq[0:68, r, sub * 128:(sub + 1) * 128], KCA[0:68, lo:hi],
                              start=True, stop=not has_patch, reads=[b_qq, b_KCA], writes=[b_SB[p]])
                        if has_patch:
                            cx.op(pe, "matmul", SB_[p][:, a_ - lo:b_ - lo], ident[:], spatch[:, a_ - pa_:b_ - pa_],
                                  start=False, stop=True, reads=[b_const, b_s4], writes=[b_SB[p]])
                        yield
                        cx.op(act, "activation", eb[:, lo:hi], SB_[p][:, 0:hi - lo], AF.Exp,
                              accum_out=lsum[p][:, 2 * r + half:2 * r + half + 1],
                              reads=[b_SB[p]], writes=[b_eb, bl])
                        yield
                    if nh == 2:
                        cx.op(dve, "tensor_tensor", lsum[p][:, 2 * r:2 * r + 1], lsum[p][:, 2 * r:2 * r + 1],
                              lsum[p][:, 2 * r + 1:2 * r + 2], ALU.add, reads=[bl], writes=[bl])
                        yield
                    cx.op(dve, "tensor_scalar", lsum[p][:, 2 * r:2 * r + 1], lsum[p][:, 2 * r:2 * r + 1], 1e-30, None, ALU.max,
                          reads=[bl], writes=[bl])
                    yield
                    cx.op(dve, "reciprocal", lsum[p][:, 2 * r:2 * r + 1], lsum[p][:, 2 * r:2 * r + 1], reads=[bl], writes=[bl])
                    yield
                    if r == 0:
                        cx.op(dve, "tensor_scalar", IMP[p][:, 1:1 + NV], eb[:, 0:NV], lsum[p][:, 0:1], None, ALU.mult,
                              reads=[b_eb, bl], writes=[bv])
                    else:
                        cx.op(dve, "scalar_tensor_tensor", IMP[p][:, 1:1 + NV], eb[:, 0:NV], lsum[p][:, 2 * r:2 * r + 1],
                              IMP[p][:, 1:1 + NV], ALU.mult, ALU.add, reads=[b_eb, bl], writes=[bv])
                    yield
                NJ = min(2 * qb + 2, NBW)
                v = lambda o: IMP[p][:, o:o + 4 * NJ].rearrange("p (j f) -> p j f", f=4)[:, :, 0]
                cx.op(dve, "tensor_tensor", PSL[p][:, 0:NJ], v(0), v(1), ALU.add, reads=[bv], writes=[bv])
                yield
                for o in (2, 3, 4):
                    cx.op(dve, "tensor_tensor", PSL[p][:, 0:NJ], PSL[p][:, 0:NJ], v(o), ALU.add, reads=[bv], writes=[bv])
                    yield
                a3, b3 = max(0, 2 * qb - 1), min(2 * qb + 2, NBW)
                o3 = a3 - (2 * qb - 1)
                cx.op(dve, "tensor_tensor", PSL[p][:, a3:b3], PSL[p][:, a3:b3], k3[:, 0, o3:o3 + b3 - a3], ALU.mult,
                      reads=[bv, b_s4], writes=[bv])
                yield
                cx.op(dve, "tensor_tensor", PSL[p][:, a3:b3], PSL[p][:, a3:b3], k3[:, 1, o3:o3 + b3 - a3], ALU.add,
                      reads=[bv, b_s4], writes=[bv])
                yield
                cx.op(dve, "memset", PSL[p][:, 0:1], BIGV, writes=[bv])
                yield
                cx.op(dve, "max", m8[p][:, 0:8], PSL[p][:, 0:NBW], reads=[bv], writes=[bv])
                yield
                cx.op(dve, "match_replace", PS2[p][:, 0:NBW], m8[p][:, 0:8], PSL[p][:, 0:NBW], -3.0e38, reads=[bv], writes=[bv])
                yield
                cx.op(dve, "max", m8[p][:, 8:16], PS2[p][:, 0:NBW], reads=[bv], writes=[bv])
                yield
                cx.op(dve, "tensor_scalar", sbq[p][:, 0:NBW], PSL[p][:, 0:NBW], m8[p][:, 15:16], NEG, ALU.is_lt, ALU.mult,
                      reads=[bv], writes=[bv])
                yield
                st_, b_st = selT[p], b_selT[p]
                pTs, b_pTs = pTs_all[p], b_pTs_all[p]

                def tail_steps():
                    for jt in range(NJT):
                        nj = min(128, NB - 128 * jt)
                        cx.op(pe, "transpose", pTs[0:nj, jt * 128:(jt + 1) * 128], sbq[p][:, jt * 128:jt * 128 + nj], ident[:],
                              reads=[bv, b_const], writes=[b_pTs])
                    yield
                    yield
                    yield
                    for jt in range(NJT):
                        nj = min(128, NB - 128 * jt)
                        cx.op(act, "copy", st_[0:nj, jt, :], pTs[0:nj, jt * 128:(jt + 1) * 128], reads=[b_pTs], writes=[b_st])
                        cx.dma(sp, s_selb[jt, 0:nj, qb * 128:(qb + 1) * 128], st_[0:nj, jt, :], src=b_st, writes=[b_scr])
                    yield
                pending_tail[p] = tail_steps

            def sel_all():
                for c in range(NCH):
                    cols = slice(c * 512, (c + 1) * 512)
                    qq, b_qq = q4[c % 2], b_q4[c % 2]
                    for r in range(4):
                        cx.dma(sp, qq[0:64, r, :], s_qn[r][:, cols], dst=b_qq, reads=[b_scr])
                        cx.dma(sp, qq[64:68, r, :], d_alq[r][:, cols], dst=b_qq)
                    yield
                    for pair in range(2):
                        gens = [qb_chain(c, 2 * pair, qq, b_qq), qb_chain(c, 2 * pair + 1, qq, b_qq)]
                        alive = [True, True]
                        while any(alive):
                            for gi_ in range(2):
                                if alive[gi_]:
                                    try:
                                        next(gens[gi_])
                                        yield
                                    except StopIteration:
                                        alive[gi_] = False
                for p_ in range(2):
                    if pending_tail[p_] is not None:
                        for _ in pending_tail[p_]():
                            yield
                        pending_tail[p_] = None

            sel_gen = sel_all()
            sel_done = [False]

            def sel_advance(n):
                for _ in range(n):
                    if sel_done[0]:
                        return
                    try:
                        next(sel_gen)
                    except StopIteration:
                        sel_done[0] = True


        def fox_unit(h, Kt, b_Kt, Vt, b_Vt):
            def aug(dT, dB):
                cx.op(pool, "memset", dT[64:70, :], 1.0, writes=[dB])
                cx.dma(sp, dT[67:70, :], s_cr[h, 3:6, :], dst=dB, reads=[b_scr])
            load_K(Kt, b_Kt, s_kf[h], aug)
            load_V(Vt, b_Vt, 64 * h)
            pipe = Pipe()
            for c in range(NCH):
                cols = slice(c * 512, (c + 1) * 512)
                i = qci[0] % 3
                qci[0] += 1
                qc, b_qc = QC[i], b_QC[i]
                cx.op(pool, "memset", qc[64:70, :], 1.0, writes=[b_qc])
                cx.dma(sp, qc[0:64, :], s_qf[h][:, cols], dst=b_qc, reads=[b_scr])
                cx.dma(sp, qc[64:67, :], s_cr[h, 0:3, cols], dst=b_qc, reads=[b_scr])
                ob = PS_O[c % 2]
                nt = 4 * c + 4
                for kt in range(nt):
                    d = kt - 4 * c
                    q0 = 128 * d if d >= 0 else 0
                    ex = []
                    if d >= 0:
                        ex.append((ident[:], tri[:, 0, :], q0, q0 + 128, [b_const]))
                    it = dict(K=70, kT=Kt[0:70, kt * 128:(kt + 1) * 128], qT=qc, q0=q0, q1=512, extras=ex,
                              reads=[b_Kt, b_qc], v=Vt[:, kt, :], vreads=[b_Vt], obank=ob, first=(kt == 0), last=(kt == nt - 1))
                    if kt == nt - 1:
                        it["epi"] = make_epilogue(ob, None, None, "fox", (h, c))
                    pipe.add(it)
                    sel_advance(2)
            pipe.flush()

        fox_unit(0, KA, b_KA, VA, b_VA)
        fox_unit(1, KBb, b_KB, VB, b_VB)

        if True:
            sel_advance(10 ** 9)
            full_barrier([b_s4] + b_dv + b_ls + b_selT + b_q4 + b_esb[0] + b_esb[1] + [b_bank[6], b_bankT])

            st4.close()

        WIN = 8
        NQW = 4
        cmask = cx.sb("cmask", [128, 5, 512], BF16)
        QW = [cx.sb(f"QW{i}", [128, WIN, 512], BF16) for i in range(NQW)]
        b_E, b_QW = B("E", True), [B(f"QW{i}", True) for i in range(NQW)]
        cx.dma(sp, cmask[:], d_cmask, dst=b_E)

        def augpos(dT, dB):
            cx.dma(sp, dT[64:68, :], d_posk, dst=dB)

        def augpos_blk(dT, dB):
            cx.dma(sp, dT[64:68, :], d_posk, dst=dB)
            cx.dma(sp, dT[68:70, :], d_blkk, dst=dB)
        load_K(KA, b_KA, s_kv[2], augpos_blk)
        load_K(KBb, b_KB, s_kv[3], augpos)
        load_V(VA, b_VA, 128)
        load_V(VB, b_VB, 192)
        pipe = Pipe([0, 1, 2, 6])
        gsi = 0
        gw = 0
        for c in range(NCH):
            if cc_hook is not None:
                cc_hook(c)
            cols = slice(c * 512, (c + 1) * 512)
            for rr in range(2):
                qc, b_qc = load_qc(s_qn[rr], d_alq[rr], 4, cols)
                gs, b_gs = gsb[gsi % 2], b_gsb[gsi % 2]
                gsi += 1
                cx.dma(sp, gs[64:65, :, :], s_gt[3 * rr:3 * rr + 3, cols].rearrange("(o r) n -> o r n", o=1), dst=b_gs, reads=[b_scr])
                nt_ = 4 * c + 4

                def prep_window(kt0):
                    nonlocal gw
                    bq = gw % NQW
                    gw += 1
                    nw = min(WIN, nt_ - kt0)
                    cx.op(dve, "tensor_copy", QW[bq][0:68, 0:nw, :], qc[0:68, :].unsqueeze(1).broadcast_to([68, nw, 512]),
                          reads=[b_qc], writes=[b_QW[bq]])
                    jt, j0 = (2 * kt0) // 128, (2 * kt0) % 128
                    cx.dma(sp, QW[bq][68:70, 0:nw, :],
                           s_selb[jt, j0:j0 + 2 * nw, cols].rearrange("(k t) n -> t k n", t=2), dst=b_QW[bq], reads=[b_scr])
                    return bq

                wbuf = {}
                for w_ in range(min(3, (nt_ + WIN - 1) // WIN)):
                    wbuf[w_] = prep_window(w_ * WIN)
                a4, b4 = c // 4, c % 4
                ob = PS_O[0]
                tiles = list(range(a4 + 1))
                for n_, nt in enumerate(tiles):
                    ex = []
                    if nt == a4:
                        ex.append((ident[:], cmask[:, b4, :], 0, 512, [b_const, b_E]))
                    elif nt == a4 - 1 and b4 == 0:
                        ex.append((ident[:], cmask[:, 4, :], 0, 512, [b_const, b_E]))
                    it = dict(K=68, kT=KCA[0:68, nt * 128:(nt + 1) * 128], qT=qc, q0=0, q1=512, extras=ex,
                              reads=[b_KCA, b_qc], v=VCA[:, nt, :], vreads=[b_VCA], obank=ob, first=(n_ == 0), last=(n_ == len(tiles) - 1))
                    if it["last"]:
                        it["epi"] = make_epilogue(ob, gs[64:65, 0, :], b_gs, "first", None)
                    pipe.add(it)
                ob = PS_O[1]
                wl = []
                for i in (3, 2, 1, 0):
                    kt = 4 * c - 4 + i
                    if kt >= 0:
                        wl.append((kt, 0, 128 * (i + 1), (128 * i, 128 * i + 128, 1)))
                for i in range(4):
                    wl.append((4 * c + i, 128 * i, 512, (128 * i, 128 * i + 128, 0)))
                for n_, (kt, q0, q1, (ma, mb, mk)) in enumerate(wl):
                    ex = [(ident[:], tri[:, mk, :], ma, mb, [b_const])]
                    it = dict(K=68, kT=KBb[0:68, kt * 128:(kt + 1) * 128], qT=qc, q0=q0, q1=q1, extras=ex,
                              reads=[b_KB, b_qc], v=VB[:, kt, :], vreads=[b_VB], obank=ob, first=(n_ == 0), last=(n_ == len(wl) - 1))
                    if it["last"]:
                        it["epi"] = make_epilogue(ob, gs[64:65, 2, :], b_gs, "mid", None)
                    pipe.add(it)
                ob = PS_O[0]
                for kt in range(nt_):
                    w_, wi_ = kt // WIN, kt % WIN
                    if wi_ == 0 and (w_ + 3) not in wbuf and (w_ + 3) * WIN < nt_:
                        wbuf[w_ + 3] = prep_window((w_ + 3) * WIN)
                    bq = wbuf[w_]
                    d = kt - 4 * c
                    q0 = 128 * d if d >= 0 else 0
                    ex = []
                    if d >= 0:
                        ex.append((ident[:], tri[:, 0, :], q0, q0 + 128, [b_const]))
                    it = dict(K=70, kT=KA[0:70, kt * 128:(kt + 1) * 128], qT=QW[bq][:, wi_, :], q0=q0, q1=512, extras=ex,
                              reads=[b_KA, b_QW[bq]], v=VA[:, kt, :], vreads=[b_VA], obank=ob, first=(kt == 0), last=(kt == nt_ - 1))
                    if it["last"]:
                        it["epi"] = make_epilogue(ob, gs[64:65, 1, :], b_gs, "last", (2 + rr, c))
                    pipe.add(it)
        pipe.flush()
        cx.global_barrier()
    ab_es.close()
    cx.sbes = cx.es

OFF = {"fq": 0, "fk": 512, "fv": 1024, "ff": 1536, "nq": 1544, "kc": 2056, "vc": 2184, "ks": 2312, "vs": 2440,
       "kw": 2568, "vw": 2696, "ng": 2824}


def ab_constants(S):
    NC = S // 16 - 1
    NCP = ((NC + 127) // 128) * 128
    d = {}
    k = np.arange(128)[:, None]
    q = np.arange(128)[None, :]
    tri = np.zeros((128, 2, 128), np.float32)
    tri[:, 0, :] = np.where(k <= q, 0.0, NEG)
    tri[:, 1, :] = np.where(k > q, 0.0, NEG)
    d["tri"] = bf16_np(tri)
    d["identb"] = bf16_np(np.eye(128, dtype=np.float32))
    d["identf"] = np.eye(128, dtype=np.float32)
    d["lstrict"] = (k < q).astype(np.float32)
    i = np.arange(128)[:, None]
    qq = np.arange(512)[None, :]
    cm = np.zeros((128, 5, 512), np.float32)
    for b in range(4):
        cm[:, b, :] = np.where(16 * i + 31 <= 512 * b + qq, 0.0, NEG)
    cm[:, 4, :] = np.where(16 * i + 31 - 2048 <= qq, 0.0, NEG)
    d["cmpmask"] = bf16_np(cm)
    jj = np.arange(10)[None, :]
    d["selpatch"] = bf16_np(np.where(16 * jj - 1 <= i, 0.0, NEG).astype(np.float32))
    k3 = np.zeros((128, 2, 3), np.float32)
    k3[:64, 0, :] = [0, 0, 0]
    k3[:64, 1, :] = [BIGV, BIGV, NEGINF]
    k3[64:, 0, :] = [1, 0, 0]
    k3[64:, 1, :] = [0, BIGV, BIGV]
    d["keepadd3"] = k3
    t = np.arange(S)
    d["blkK"] = bf16_np(np.stack([(t % 128) < 64, (t % 128) >= 64]).astype(np.float32))
    d["posK"] = bf16_np(np.stack([128.0 * (t // 128), 1.0 * (t % 128), np.ones(S), np.ones(S)]).astype(np.float32))
    ce = 16 * np.arange(NCP) + 31
    d["cmpK"] = bf16_np(np.stack([128.0 * (ce // 128), 1.0 * (ce % 128), np.ones(NCP), np.ones(NCP)]).astype(np.float32))
    return d


def ab_core_inputs(inp, b, j, S, consts):
    w = inp["attn_w_in"][0]
    g = j // 2
    own = [2 * j, 2 * j + 1]
    others = [h for h in range(4 * g, 4 * g + 4) if h not in own]
    qn_order = own + others
    cols = []
    sl = lambda base, h: list(range(OFF[base] + 64 * h, OFF[base] + 64 * h + 64))
    cols += sl("fq", own[0]) + sl("fq", own[1]) + sl("fk", own[0]) + sl("fk", own[1])
    for h in qn_order:
        cols += sl("nq", h)
    cols += sl("kc", g) + sl("vc", g) + sl("ks", g) + sl("kw", g)
    cols += [OFF["ff"] + own[0], OFF["ff"] + own[1]]
    for h in own:
        cols += [OFF["ng"] + 3 * h + r for r in range(3)]
    assert len(cols) == NFM
    wfm = w[:, cols].reshape(8, 128, NFM).transpose(1, 0, 2)
    colt = sl("fv", own[0]) + sl("fv", own[1]) + sl("vs", g) + sl("vw", g)
    wtm = w[:, colt].reshape(8, 128, 256).transpose(1, 0, 2)
    d = dict(consts)
    d["x"] = np.ascontiguousarray(inp["x"][b, :S])
    d["g0"] = np.ascontiguousarray(inp["norm_g"][0, 0])
    d["wfm"] = np.ascontiguousarray(wfm)
    d["wtm"] = np.ascontiguousarray(wtm)
    d["bf"] = np.ascontiguousarray(inp["fox_b_f"][0, own].reshape(2, 1))
    pe_ = inp["nsa_cmp_pe"][0]
    d["cpe"] = np.ascontiguousarray(pe_.transpose(0, 2, 1))
    w1 = inp["nsa_cmp_w1"][0].reshape(2, 32, 64, 64)
    d["cw1"] = np.ascontiguousarray(w1.transpose(0, 2, 1, 3).reshape(2, 64, 2048))
    d["cw2"] = np.ascontiguousarray(inp["nsa_cmp_w2"][0])
    t = np.arange(S)
    al = np.zeros((4, 4, S), np.float32)
    for n_, h in enumerate(qn_order):
        s_ = SLOPES[h]
        al[n_, 0] = s_
        al[n_, 1] = s_
        al[n_, 2] = -s_ * 128.0 * (t // 128)
        al[n_, 3] = -s_ * (t % 128)
    d["alibiQ"] = bf16_np(al)
    return d


def ab_slot_features(j):
    return [64 * (2 * j), 64 * (2 * j + 1), 512 + 64 * (2 * j), 512 + 64 * (2 * j + 1)]


U32 = mybir.dt.uint32


def build_fused(S):
    TPC = S // 4
    NPS = TPC // 512
    groups = [1] + [4] * NPS
    NPC = NPS + 1
    pw = [128] + [512] * NPS
    nc = bass.Bass("TRN2", target_bir_lowering=False)
    src_t = [nc.dram_tensor(f"cc_src{k}", [256, 4 * pw[k]], BF16) for k in range(NPC)]
    gat_t = [nc.dram_tensor(f"cc_gat{k}", [1024, 4 * pw[k]], BF16) for k in range(NPC)]
    d_idx = nc.dram_tensor("oidx", [128, 8], U32, kind="ExternalInput").ap()
    cx = Ctx(nc)
    with cx.es:
        sp, pool = cx.sp, cx.pool
        pbank = [cx.ps(f"bank{i}", [128, 512], F32) for i in range(7)]
        b_bank = [cx.buf(f"bank{i}") for i in range(7)]
        pbT = cx.ps("bankT", [128, 1024], BF16)
        b_bankT = cx.buf("bankT")
        zt = cx.sb("zt", [128, 128], BF16)
        idxsb = cx.sb("idxsb", [128, 8], U32)
        b_zt, b_idx = cx.buf("zt", True), cx.buf("idx", True)
        b_cc = cx.buf("ccsrc")
        src3 = [t.ap().rearrange("f (s w) -> f s w", s=4) for t in src_t]
        cx.op(pool, "memset", zt[:], 0.0, writes=[b_zt])
        cx.dma(sp, src3[0][0:128, 0, :], zt[:], src=b_zt, writes=[b_cc])
        cx.dma(sp, src3[0][128:256, 0, :], zt[:], src=b_zt, writes=[b_cc])
        cx.dma(sp, idxsb[:], d_idx, dst=b_idx)

        def out_writer(cx_, obf_t, b_obf_, key):
            slot, c = key
            j, cl = c // NPS, c % NPS
            cx_.dma(sp, src3[1 + cl][slot * 64:(slot + 1) * 64, j, :], obf_t[:], src=b_obf_, writes=[b_cc])
            if cl == NPS - 1 and j + 1 < 4:
                cx_.dma(sp, src3[0][slot * 64:(slot + 1) * 64, j + 1, :], obf_t[:, 384:512], src=b_obf_, writes=[b_cc])

        D = c_drams(nc, sum(groups) * 128, (sum(groups) - 1) * 128)
        emit_prepass(nc, cx, D)
        csem = cx.new_sem("ccsem")
        obf_bufs = []
        issued = []

        def issue_cc(k):
            toks = [Tok(b.sem, b.cnt) for b in obf_bufs if b.cnt > 0] + [Tok(b_zt.sem, b_zt.cnt)]
            pool.wait(toks)
            nc.gpsimd.collective_compute("AllGather", ALU.bypass, replica_groups=[[0, 1, 2, 3], [4, 5, 6, 7]],
                                         ins=[src_t[k].ap().opt()], outs=[gat_t[k].ap().opt()]).then_inc(csem)
            issued.append(k)

        def cc_hook(c):
            done_c = c - 2
            if done_c == 3 * NPS - 1 and 0 not in issued:
                issue_cc(0)
            if done_c >= 3 * NPS:
                k = 1 + (done_c - 3 * NPS)
                if k not in issued:
                    issue_cc(k)

        _ow = out_writer

        def out_writer2(cx_, obf_t, b_obf_, key):
            if b_obf_ not in obf_bufs:
                obf_bufs.append(b_obf_)
            _ow(cx_, obf_t, b_obf_, key)

        emit_AB(nc, cx, S, pbank, b_bank, pbT, b_bankT, out_writer2, cc_hook)
        def ensure_cc(upto):
            for k in range(min(upto, NPC - 1) + 1):
                if k not in issued:
                    issue_cc(k)
        ensure_cc(1)
        gat_rows = [t.ap().rearrange("f (s w) -> (f s) w", s=4) for t in gat_t]
        cc_waited = [0]

        def oT_loader(cx_, oTg, b_oTg, tok0, N):
            k = 0 if tok0 == 0 else 1 + (tok0 - 128) // 512
            ensure_cc(k + 2)
            need = issued.index(k) + 1
            if cc_waited[0] < need:
                nc.gpsimd.wait_ge(csem, need)
                cc_waited[0] = need
            for cg in range(8):
                cx_.dma(pool, oTg[:, cg, :N], gat_rows[k], dst=b_oTg, reads=[b_idx],
                        indirect=bass.IndirectOffsetOnAxis(idxsb[:, cg:cg + 1], 0))

        emit_C(nc, cx, groups, 1, pbank, b_bank, pbT, b_bankT, oT_loader, D)
    return nc


_PROG = {}


def kernel(**inp):
    inp = {k_: np.asarray(v) for k_, v in inp.items()}
    Bn, S, D = inp["x"].shape
    TPC = S // 4
    consts = ab_constants(S)
    if S not in _PROG:
        _PROG[S] = build_fused(S)
    wi = c_weight_inputs(inp)
    perm = [(cg // 2) if cg % 2 == 0 else 4 + cg // 2 for cg in range(8)]
    wi["wout"] = np.ascontiguousarray(wi["wout"][:, perm, :])
    in_maps = []
    for c in range(8):
        b, j = c // 4, c % 4
        m = ab_core_inputs(inp, b, j, S, consts)
        m.update(wi)
        t0 = TPC * j - 128
        ntok = TPC + 128
        xt = np.zeros((ntok, 1024), np.float32)
        lo = max(t0, 0)
        xt[lo - t0:] = inp["x"][b, lo:t0 + ntok]
        m["xt"] = xt
        m["invtab"] = inv_table(j == 0)
        p = np.arange(128)[:, None]
        cg = np.arange(8)[None, :]
        m["oidx"] = ((cg * 128 + p) * 4 + j).astype(np.uint32)
        in_maps.append(m)
    res = run_bass_kernel_spmd(_PROG[S], in_maps, core_ids=list(range(8)))
    out = np.zeros((Bn, S, D), np.float32)
    for c in range(8):
        b, j = c // 4, c % 4
        out[b, TPC * j:TPC * (j + 1)] = res.results[c]["out"]
    return out
```
